# Optimizing a Trainium2 kernel written in Bass

```python
import jax, jax.numpy as jnp
from jax import lax
import numpy as np

D_MODEL = 1024
BATCH = 8
SEQ = 8192
DEPTH = 2

GRID_W = 64
CTX_LEN = 256
N_EVEN = (DEPTH + 1) // 2
N_ODD = DEPTH // 2
POOL_GROUPS = 4
POOL_WINDOWS = (2, 4, 8, 16)
POOL_DIM = D_MODEL // 2
POOL_GDIM = POOL_DIM // POOL_GROUPS
HEAD_DIM = 64
N_Q_HEADS = (D_MODEL // 2) // HEAD_DIM
N_KV_HEADS = 2
GQA_GROUP = N_Q_HEADS // N_KV_HEADS
Q_DIM = N_Q_HEADS * HEAD_DIM
KV_DIM = N_KV_HEADS * HEAD_DIM
WINDOW = 128
BLOCK = 128
ROPE_BASE = 10000.0
ROPE_HALF = HEAD_DIM // 2
IN_DIM = POOL_DIM + Q_DIM + 2 * KV_DIM
MIX_DIM = POOL_DIM + Q_DIM
FOURIER_GROUPS = 4
FOURIER_GDIM = D_MODEL // FOURIER_GROUPS
D_FF = 2816
N_MOD = 9
ALPHA = (2 * DEPTH) ** 0.25
BETA = (8 * DEPTH) ** -0.25
LN_EPS = 1e-5

kernel_name = "hybrid_pool_swa_fourier_macaron_dit"


def layer_norm(x, g=None, b=None):
    x32 = x.astype(jnp.float32)
    mu = jnp.mean(x32, axis=-1, keepdims=True)
    var = jnp.mean(jnp.square(x32 - mu), axis=-1, keepdims=True)
    y = (x32 - mu) * lax.rsqrt(var + LN_EPS)
    if g is not None:
        y = y * g.astype(jnp.float32) + b.astype(jnp.float32)
    return y.astype(x.dtype)


def modulate(x, shift, scale):
    return layer_norm(x) * (1 + scale) + shift


def swiglu(h, w1, w3, w2):
    return (jax.nn.silu(h @ w1) * (h @ w3)) @ w2


def rotate(t, ang):
    cos = jnp.cos(ang)[:, None, :].astype(t.dtype)
    sin = jnp.sin(ang)[:, None, :].astype(t.dtype)
    t1, t2 = jnp.split(t, 2, axis=-1)
    return jnp.concatenate([t1 * cos - t2 * sin, t1 * sin + t2 * cos], axis=-1)


def rope_2d(t, ang_row, ang_col):
    return jnp.concatenate([rotate(t[..., :ROPE_HALF], ang_row), rotate(t[..., ROPE_HALF:], ang_col)], axis=-1)


def pool_mixer(p, w_pool, pool_scale):
    b_, L, _ = p.shape
    p32 = p.astype(jnp.float32).reshape(b_, L, POOL_GROUPS, POOL_GDIM)
    csum = jnp.concatenate([jnp.zeros_like(p32[:, :1]), jnp.cumsum(p32, axis=1)], axis=1)
    t = jnp.arange(L)
    outs = []
    for g, w in enumerate(POOL_WINDOWS):
        lo = jnp.clip(t - w // 2, 0, L)
        hi = jnp.clip(t + w // 2, 0, L)
        cs = csum[:, :, g]
        win_sum = jnp.take(cs, hi, axis=1) - jnp.take(cs, lo, axis=1)
        mean = win_sum / (hi - lo).astype(jnp.float32)[None, :, None]
        outs.append((mean - p32[:, :, g]).astype(p.dtype) @ w_pool[g])
    return jnp.concatenate(outs, axis=-1) * pool_scale


def banded_gqa(q, k, v, k_ctx, v_ctx, sinks):
    b_, L = q.shape[:2]
    nb = L // BLOCK
    scale = HEAD_DIM ** -0.5
    qb = q.reshape(b_, nb, BLOCK, N_KV_HEADS, GQA_GROUP, HEAD_DIM)

    def band(t):
        tp = jnp.pad(t, ((0, 0), (BLOCK, BLOCK), (0, 0), (0, 0))).reshape(b_, nb + 2, BLOCK, N_KV_HEADS, HEAD_DIM)
        return jnp.concatenate([tp[:, :-2], tp[:, 1:-1], tp[:, 2:]], axis=2)

    kb, vb = band(k), band(v)
    s_loc = jnp.einsum('bnqhgd,bnkhd->bnhgqk', qb, kb).astype(jnp.float32) * scale
    s_ctx = jnp.einsum('bnqhgd,bchd->bnhgqc', qb, k_ctx).astype(jnp.float32) * scale
    qi = jnp.arange(BLOCK)[:, None]
    ki = jnp.arange(3 * BLOCK)[None, :]
    key_pos = jnp.arange(nb)[:, None, None] * BLOCK - BLOCK + ki[None]
    valid = (jnp.abs(ki - BLOCK - qi) <= WINDOW)[None] & (key_pos >= 0) & (key_pos < L)
    s_loc = jnp.where(valid[None, :, None, None], s_loc, -jnp.inf)
    sink = sinks.astype(jnp.float32).reshape(1, 1, N_KV_HEADS, GQA_GROUP, 1, 1)
    m = jnp.maximum(jnp.maximum(jnp.max(s_loc, axis=-1, keepdims=True), jnp.max(s_ctx, axis=-1, keepdims=True)), sink)
    e_loc = jnp.exp(s_loc - m)
    e_ctx = jnp.exp(s_ctx - m)
    denom = jnp.sum(e_loc, axis=-1, keepdims=True) + jnp.sum(e_ctx, axis=-1, keepdims=True) + jnp.exp(sink - m)
    out = (jnp.einsum('bnhgqk,bnkhd->bnhgqd', e_loc, vb.astype(jnp.float32))
           + jnp.einsum('bnhgqc,bchd->bnhgqd', e_ctx, v_ctx.astype(jnp.float32))) / denom
    out = jnp.transpose(out, (0, 1, 4, 2, 3, 5)).astype(q.dtype)
    return out.reshape(b_, L, Q_DIM)


def mixer_pool_attn(h, h_ctx, w_in, w_pool, pool_scale, sinks, w_out, ang_row, ang_col):
    b_, L, _ = h.shape
    u = h @ w_in
    p, q, k, v = jnp.split(u, [POOL_DIM, POOL_DIM + Q_DIM, POOL_DIM + Q_DIM + KV_DIM], axis=-1)
    kv_ctx = h_ctx @ w_in[:, POOL_DIM + Q_DIM:]
    k_ctx, v_ctx = jnp.split(kv_ctx, 2, axis=-1)
    n_ctx = h_ctx.shape[1]
    q = rope_2d(q.reshape(b_, L, N_Q_HEADS, HEAD_DIM), ang_row, ang_col)
    k = rope_2d(k.reshape(b_, L, N_KV_HEADS, HEAD_DIM), ang_row, ang_col)
    v = v.reshape(b_, L, N_KV_HEADS, HEAD_DIM)
    k_ctx = k_ctx.reshape(b_, n_ctx, N_KV_HEADS, HEAD_DIM)
    v_ctx = v_ctx.reshape(b_, n_ctx, N_KV_HEADS, HEAD_DIM)
    a = pool_mixer(p, w_pool, pool_scale)
    o = banded_gqa(q, k, v, k_ctx, v_ctx, sinks)
    return jnp.concatenate([a, o], axis=-1) @ w_out


def fourier_mixer(h, w_out):
    b_, L, _ = h.shape
    hg = h.astype(jnp.float32).reshape(b_, L, FOURIER_GROUPS, FOURIER_GDIM)
    f = jnp.fft.fft2(hg, axes=(1, 3), norm='ortho').real
    return f.reshape(b_, L, D_MODEL).astype(h.dtype) @ w_out


def setup_inputs(seed: int = 0) -> dict:
    key = jax.random.key(seed)
    ks = jax.random.split(key, 20)

    def nrm(k, shape, s):
        return jax.random.normal(k, shape, jnp.float32) * s

    return {
        'x': nrm(ks[0], (BATCH, SEQ, D_MODEL), 1.0),
        'c': nrm(ks[1], (BATCH, D_MODEL), 1.0),
        'ctx': nrm(ks[2], (BATCH, CTX_LEN, D_MODEL), 1.0),
        'c_ctx': nrm(ks[3], (D_MODEL,), 1.0),
        'ada_w': nrm(ks[4], (DEPTH, D_MODEL, N_MOD * D_MODEL), 0.5 * D_MODEL ** -0.5),
        'ada_b': nrm(ks[5], (DEPTH, N_MOD * D_MODEL), 0.02),
        'ln_g': 1.0 + nrm(ks[6], (DEPTH, 3, D_MODEL), 0.02),
        'ln_b': nrm(ks[7], (DEPTH, 3, D_MODEL), 0.02),
        'ffn1_w1': nrm(ks[8], (DEPTH, D_MODEL, D_FF), D_MODEL ** -0.5),
        'ffn1_w3': nrm(ks[9], (DEPTH, D_MODEL, D_FF), D_MODEL ** -0.5),
        'ffn1_w2': nrm(ks[10], (DEPTH, D_FF, D_MODEL), BETA * D_FF ** -0.5),
        'ffn2_w1': nrm(ks[11], (DEPTH, D_MODEL, D_FF), D_MODEL ** -0.5),
        'ffn2_w3': nrm(ks[12], (DEPTH, D_MODEL, D_FF), D_MODEL ** -0.5),
        'ffn2_w2': nrm(ks[13], (DEPTH, D_FF, D_MODEL), BETA * D_FF ** -0.5),
        'mix_w_in': nrm(ks[14], (N_EVEN, D_MODEL, IN_DIM), D_MODEL ** -0.5),
        'pool_w': nrm(ks[15], (N_EVEN, POOL_GROUPS, POOL_GDIM, POOL_GDIM), POOL_GDIM ** -0.5),
        'pool_scale': 1.0 + nrm(ks[16], (N_EVEN, POOL_DIM), 0.02),
        'attn_sinks': nrm(ks[17], (N_EVEN, N_Q_HEADS), 0.5),
        'mix_w_out': nrm(ks[18], (N_EVEN, MIX_DIM, D_MODEL), BETA * MIX_DIM ** -0.5),
        'fourier_w_out': nrm(ks[19], (N_ODD, D_MODEL, D_MODEL), BETA * D_MODEL ** -0.5),
    }


def reference(x, c, ctx, c_ctx, ada_w, ada_b, ln_g, ln_b, ffn1_w1, ffn1_w3, ffn1_w2, ffn2_w1, ffn2_w3, ffn2_w2,
              mix_w_in, pool_w, pool_scale, attn_sinks, mix_w_out, fourier_w_out):
    L = x.shape[1]
    ROWS = L // GRID_W
    row = jnp.repeat(jnp.arange(ROWS), GRID_W).astype(jnp.float32)
    col = jnp.tile(jnp.arange(GRID_W), ROWS).astype(jnp.float32)
    freqs = ROPE_BASE ** (-jnp.arange(0, ROPE_HALF, 2, dtype=jnp.float32) / ROPE_HALF)
    ang_row = row[:, None] * freqs[None]
    ang_col = col[:, None] * freqs[None]
    silu_c = jax.nn.silu(c)
    silu_cc = jax.nn.silu(c_ctx)
    for l in range(DEPTH):
        mod = (silu_c @ ada_w[l] + ada_b[l])[:, None, :]
        sh1, sc1, g1, sh2, sc2, g2, sh3, sc3, g3 = jnp.split(mod, N_MOD, axis=-1)
        y = swiglu(modulate(x, sh1, sc1), ffn1_w1[l], ffn1_w3[l], ffn1_w2[l])
        x = layer_norm(ALPHA * x + 0.5 * g1 * y, ln_g[l, 0], ln_b[l, 0])
        if l % 2 == 0:
            e = l // 2
            mod_c = (silu_cc @ ada_w[l][:, :5 * D_MODEL] + ada_b[l][:5 * D_MODEL])[None, None, :]
            csh1, csc1, cg1, csh2, csc2 = jnp.split(mod_c, 5, axis=-1)
            yc = swiglu(modulate(ctx, csh1, csc1), ffn1_w1[l], ffn1_w3[l], ffn1_w2[l])
            ctx = layer_norm(ALPHA * ctx + 0.5 * cg1 * yc, ln_g[l, 0], ln_b[l, 0])
            mix = mixer_pool_attn(modulate(x, sh2, sc2), modulate(ctx, csh2, csc2), mix_w_in[e], pool_w[e],
                                  pool_scale[e], attn_sinks[e], mix_w_out[e], ang_row, ang_col)
        else:
            mix = fourier_mixer(modulate(x, sh2, sc2), fourier_w_out[l // 2])
        x = layer_norm(ALPHA * x + g2 * mix, ln_g[l, 1], ln_b[l, 1])
        y = swiglu(modulate(x, sh3, sc3), ffn2_w1[l], ffn2_w3[l], ffn2_w2[l])
        x = layer_norm(ALPHA * x + 0.5 * g3 * y, ln_g[l, 2], ln_b[l, 2])
    return x
```

```python
import contextlib
import numpy as np
import ml_dtypes
import concourse.bass as bass
import concourse.mybir as mybir
from concourse.bass_utils import run_bass_kernel_spmd

F32 = mybir.dt.float32
BF16 = mybir.dt.bfloat16
AF = mybir.ActivationFunctionType
ALU = mybir.AluOpType
NPBF = ml_dtypes.bfloat16

D = 1024
L = 8192
FF = 2816
NFF = 22
NCH = 8
CTX = 256
NB = 64
ALPHA = 4.0 ** 0.25
EPS = 1e-5
EPS_A = EPS / (ALPHA * ALPHA)
NEG = -30000.0


class Eng:
    def __init__(self, nc, name, eng, is_pe=False):
        self.name = name
        self.e = eng
        self.is_pe = is_pe
        self.sem = nc.alloc_semaphore(name="es_" + name)
        self.cnt = 0
        self.waited = {}

    def wait(self, ev):
        key, sem, val, _ = ev
        if self.waited.get(key, 0) >= val:
            return
        self.e.wait_ge(sem, val)
        self.waited[key] = val


class Buf:
    def __init__(self, name):
        self.name = name
        self.w = None
        self.r = {}


class DSem:
    def __init__(self, nc, name):
        self.name = name
        self.sem = nc.alloc_semaphore(name="ds_" + name)
        self.cnt = 0


class Kern:
    def __init__(self, nc):
        self.nc = nc
        self.pe = Eng(nc, "pe", nc.tensor, is_pe=True)
        self.act = Eng(nc, "act", nc.scalar)
        self.dve = Eng(nc, "dve", nc.vector)
        self.pool = Eng(nc, "pool", nc.gpsimd)
        self.sp = Eng(nc, "sp", nc.sync)
        self.engs = [self.pe, self.act, self.dve, self.pool, self.sp]
        self.stores = {}
        self.dsems = {}

    def dsem(self, name):
        if name not in self.dsems:
            self.dsems[name] = DSem(self.nc, name)
        return self.dsems[name]

    def _deps(self, eng, reads, writes):
        for b in reads:
            if b.w is not None:
                self._need(eng, b.w, True)
        for b in writes:
            if b.w is not None:
                self._need(eng, b.w, False)
            for ev in b.r.values():
                self._need(eng, ev, False)

    def _need(self, eng, ev, raw):
        src = ev[3]
        if src is eng:
            if eng.is_pe or not raw:
                return
        eng.wait(ev)

    def op(self, eng, fn, reads=(), writes=()):
        self._deps(eng, reads, writes)
        ins = fn()
        eng.cnt += 1
        ins.then_inc(eng.sem, 1)
        ev = (eng.name, eng.sem, eng.cnt, eng)
        for b in reads:
            b.r[eng.name] = ev
        for b in writes:
            b.w = ev
            b.r = {}
        return ev

    def dma(self, q, pairs, ds, reads=(), writes=(), store=False, **kw):
        self._deps(q, reads, writes)
        for (o, i) in pairs:
            ins = q.e.dma_start(out=o, in_=i, **kw)
            ds.cnt += 16
            ins.then_inc(ds.sem, 16)
        ev = (ds.name, ds.sem, ds.cnt, None)
        for b in reads:
            b.r[ds.name] = ev
        for b in writes:
            b.w = ev
            b.r = {}
        if store:
            self.stores[ds.name] = ev
        return ev

    def barrier(self, dram_bufs=()):
        sp = self.sp
        for ev in self.stores.values():
            sp.wait(ev)
        self.stores = {}
        for y in self.engs:
            if y is not sp and y.cnt > 0:
                sp.wait((y.name, y.sem, y.cnt, y))
        sp.e.sem_inc(sp.sem, 1)
        sp.cnt += 1
        ev = (sp.name, sp.sem, sp.cnt, None)
        for x in self.engs:
            if x is not sp:
                x.wait(ev)
        for b in dram_bufs:
            b.w = None
            b.r = {}


class T:
    def __init__(self, t, name):
        self.t = t
        self.b = Buf(name)

    def __getitem__(self, idx):
        return self.t[idx]


class Phase:
    count = 0

    def __init__(self, k):
        self.k = k
        self.nc = k.nc
        self.st = contextlib.ExitStack()
        Phase.count += 1
        self.tag = f"p{Phase.count}_"

    def sb(self, name, shape, dt):
        return T(self.st.enter_context(self.nc.sbuf_tensor(self.tag + name, shape, dt)), name)

    def ps(self, name, shape, dt):
        return T(self.st.enter_context(self.nc.psum_tensor(self.tag + name, shape, dt)), name)

    def close(self):
        self.st.close()


def ln_stats(k, x, st6, mv, rstd, half, eps):
    nc = k.nc
    k.op(k.dve, lambda: nc.vector.bn_stats(out=st6[:, 0, :], in_=x[:, 0:512]), [x.b], [st6.b])
    yield
    k.op(k.dve, lambda: nc.vector.bn_stats(out=st6[:, 1, :], in_=x[:, 512:1024]), [x.b], [st6.b])
    yield
    k.op(k.dve, lambda: nc.vector.bn_aggr(out=mv[:, :], in_=st6[:, :, :]), [st6.b], [mv.b])
    k.op(k.pool, lambda: nc.gpsimd.tensor_scalar_add(out=rstd[:, :], in0=mv[:, 1:2], scalar1=eps), [mv.b], [rstd.b])
    k.op(k.pool, lambda: nc.gpsimd.tensor_tensor(out=rstd[:, :], in0=rstd[:, :], in1=half[:, :], op=ALU.pow),
         [rstd.b, half.b], [rstd.b])
    yield


def load_mod_vectors(k, ph, modd, l, r, idx_sh, idx_sc, shP, scP):
    nc = k.nc
    q = k.sp
    src_sh = modd[l, r, idx_sh * D:(idx_sh + 1) * D].rearrange("(c p) -> p c", p=128)
    src_sc = modd[l, r, idx_sc * D:(idx_sc + 1) * D].rearrange("(c p) -> p c", p=128)
    k.dma(q, [(shP[:, :], src_sh)], k.dsem(shP.b.name), [], [shP.b], allow_slow_non_contiguous=True)
    k.dma(q, [(scP[:, :], src_sc)], k.dsem(scP.b.name), [], [scP.b], allow_slow_non_contiguous=True)
    k.op(k.dve, lambda: nc.vector.tensor_scalar_add(out=scP[:, :], in0=scP[:, :], scalar1=1.0), [scP.b], [scP.b])


def load_bcast(k, dst, src_row, scale=None):
    nc = k.nc
    k.dma(k.sp, [(dst[:, :], src_row.broadcast_to([128, D]))], k.dsem(dst.b.name), [], [dst.b])
    if scale is not None:
        k.op(k.dve, lambda: nc.vector.tensor_scalar_mul(out=dst[:, :], in0=dst[:, :], scalar1=float(scale)),
             [dst.b], [dst.b])


def run(gen):
    for _ in gen:
        pass


class BG:
    def __init__(self):
        self.q = []

    def add(self, gen):
        self.q.append(gen)

    def step(self, n=1):
        while n > 0 and self.q:
            try:
                next(self.q[0])
                n -= 1
            except StopIteration:
                self.q.pop(0)

    def flush(self):
        while self.q:
            run(self.q.pop(0))


def prologue_a(k, x, st6, mv, rstd, half, xn):
    nc = k.nc
    yield from ln_stats(k, x, st6, mv, rstd, half, EPS)
    k.op(k.dve, lambda: nc.vector.tensor_scalar(out=xn[:, :], in0=x[:, :], scalar1=mv[:, 0:1], scalar2=rstd[:, 0:1],
                                                op0=ALU.subtract, op1=ALU.mult), [x.b, mv.b, rstd.b], [xn.b])
    yield


def prologue_b(k, xn, psT, ident, hT, col0, shP, scP):
    nc = k.nc
    for c in range(NCH):
        k.op(k.pe, lambda c=c: nc.tensor.transpose(out=psT[:, c, :], in_=xn[:, c * 128:(c + 1) * 128], identity=ident[:, :]),
             [xn.b, ident.b], [psT.b])
    for c in range(NCH):
        k.op(k.act, lambda c=c: nc.scalar.activation(out=hT[:, c, col0:col0 + 128], in_=psT[:, c, :], func=AF.Identity,
                                                     scale=scP[:, c:c + 1], bias=shP[:, c:c + 1]),
             [psT.b, scP.b, shP.b], [hT.b])


def epilogue_sub(k, pys, xres, gb, gamb, betb, r, st6, mv, rstd, half, out_ap, store_name, store_pairs=None):
    nc = k.nc
    for n in range(2):
        k.op(k.dve, lambda n=n: nc.vector.tensor_tensor(out=r[:, n * 512:(n + 1) * 512], in0=pys[n][:, :],
                                                        in1=gb[:, n * 512:(n + 1) * 512], op=ALU.mult),
             [pys[n].b, gb.b], [r.b])
        yield
    k.op(k.pool, lambda: nc.gpsimd.tensor_tensor(out=r[:, :], in0=r[:, :], in1=xres[:, :], op=ALU.add),
         [r.b, xres.b], [r.b])
    yield from ln_stats(k, r, st6, mv, rstd, half, EPS_A)
    k.op(k.dve, lambda: nc.vector.tensor_scalar(out=r[:, :], in0=r[:, :], scalar1=mv[:, 0:1], scalar2=rstd[:, 0:1],
                                                op0=ALU.subtract, op1=ALU.mult), [r.b, mv.b, rstd.b], [r.b])
    yield
    k.op(k.pool, lambda: nc.gpsimd.tensor_tensor(out=r[:, :], in0=r[:, :], in1=gamb[:, :], op=ALU.mult),
         [r.b, gamb.b], [r.b])
    k.op(k.pool, lambda: nc.gpsimd.tensor_tensor(out=r[:, :], in0=r[:, :], in1=betb[:, :], op=ALU.add),
         [r.b, betb.b], [r.b])
    k.dma(k.sp, store_pairs or [(out_ap, r[:, :])], k.dsem(store_name), [r.b], [], store=True)


def phase_mod(k, io):
    nc = k.nc
    ph = Phase(k)
    cT = ph.sb("cT", [128, NCH, 2], F32)
    cs = ph.sb("cs", [128, NCH, 2], F32)
    wb = [ph.sb(f"adaw{i}", [128, NCH, 512], F32) for i in range(2)]
    ab = ph.sb("adab", [2, 9216], F32)
    mrow = ph.sb("mrow", [2, 9216], F32)
    pm = [ph.ps(f"pmod{i}", [2, 512], F32) for i in range(2)]
    k.dma(k.sp, [(cT[:, :, :], io["cvecT"])], k.dsem("cT"), [], [cT.b])
    k.op(k.act, lambda: nc.scalar.activation(out=cs[:, :, :], in_=cT[:, :, :], func=AF.Silu), [cT.b], [cs.b])
    it = 0
    for l in range(2):
        k.dma(k.sp, [(ab[:, :], io["ada_b"][l:l + 1, :].broadcast_to([2, 9216]))], k.dsem("adab"), [], [ab.b])
        wv = io["ada_w"][l].rearrange("(c p) n -> p c n", p=128)
        for n in range(18):
            w = wb[it % 2]
            p = pm[it % 2]
            k.dma(k.sp, [(w[:, 0:4, :], wv[:, 0:4, n * 512:(n + 1) * 512]),
                         (w[:, 4:8, :], wv[:, 4:8, n * 512:(n + 1) * 512])], k.dsem(w.b.name), [], [w.b])
            for c in range(NCH):
                k.op(k.pe, lambda c=c, w=w, p=p: nc.tensor.matmul(p[:, :], lhsT=cs[:, c, :], rhs=w[:, c, :],
                                                                 start=(c == 0), stop=(c == NCH - 1)),
                     [cs.b, w.b], [p.b])
            k.op(k.dve, lambda n=n, p=p: nc.vector.tensor_tensor(out=mrow[:, n * 512:(n + 1) * 512], in0=p[:, :],
                                                                in1=ab[:, n * 512:(n + 1) * 512], op=ALU.add),
                 [p.b, ab.b], [mrow.b])
            it += 1
        k.dma(k.sp, [(io["modd"][l], mrow[:, :])], k.dsem("mrow"), [mrow.b], [], store=True)
    k.barrier()
    ph.close()


def phase_ffn(k, io, l, W1, W3, W2, msh, msc, mg, ln_i, xin, xout, ctx_io=None):
    nc = k.nc
    ph = Phase(k)
    TT = 256
    w1s = ph.sb("w1s", [128, NCH, FF], BF16)
    w3s = ph.sb("w3s", [128, NCH, FF], BF16)
    w2s = ph.sb("w2s", [128, NFF, D], BF16)
    ident = ph.sb("ident", [128, 128], BF16)
    half = ph.sb("half", [128, 1], F32)
    gbs = [ph.sb("gb0", [128, D], F32)]
    gamb = ph.sb("gamb", [128, D], F32)
    betb = ph.sb("betb", [128, D], F32)
    shPs = [ph.sb("shP0", [128, NCH], F32)]
    scPs = [ph.sb("scP0", [128, NCH], F32)]
    if ctx_io is not None:
        gbs.append(ph.sb("gb1", [128, D], F32))
        shPs.append(ph.sb("shP1", [128, NCH], F32))
        scPs.append(ph.sb("scP1", [128, NCH], F32))
    hT = [ph.sb(f"hT{i}", [128, NCH, TT], BF16) for i in range(2)]
    u = ph.sb("u", [128, NFF, TT], BF16)
    ubufs = [Buf(f"u{j}") for j in range(NFF)]
    xs = [[ph.sb(f"xs{i}{s}", [128, D], F32) for s in range(2)] for i in range(2)]
    xn = [ph.sb(f"xn{s}", [128, D], BF16) for s in range(2)]
    rr = [[ph.sb(f"r{i}{s}", [128, D], F32) for s in range(2)] for i in range(2)]
    sg = [ph.sb(f"sg{i}", [128, TT], F32) for i in range(2)]
    st6 = [ph.sb(f"st6{i}", [128, 2, 6], F32) for i in range(4)]
    mv = [ph.sb(f"mv{i}", [128, 2], F32) for i in range(4)]
    rstd = [ph.sb(f"rstd{i}", [128, 1], F32) for i in range(4)]
    psT = [ph.ps(f"psT{i}", [128, NCH, 128], BF16) for i in range(2)]
    ps13 = [ph.ps(f"ps13{i}", [128, 512], F32) for i in range(2)]
    psy = [ph.ps(f"psy{i}", [128, 512], F32) for i in range(4)]

    k.dma(k.sp, [(ident[:, :], io["ident"])], k.dsem("ident"), [], [ident.b])
    k.op(k.dve, lambda: nc.vector.memset(half[:, :], -0.5), [], [half.b])
    modd = io["modd"]
    for r_ in range(len(gbs)):
        load_mod_vectors(k, ph, modd, l, r_, msh, msc, shPs[r_], scPs[r_])
        load_bcast(k, gbs[r_], modd[l, r_:r_ + 1, mg * D:(mg + 1) * D], 0.5 / ALPHA)
    load_bcast(k, gamb, io["ln_g"][l, ln_i:ln_i + 1, :])
    load_bcast(k, betb, io["ln_b"][l, ln_i:ln_i + 1, :])
    W1v = W1.rearrange("(c p) n -> p c n", p=128)
    W3v = W3.rearrange("(c p) n -> p c n", p=128)
    W2v = W2.rearrange("(j p) n -> p j n", p=128)
    for c in range(NCH):
        k.dma(k.pool, [(w1s[:, c, :], W1v[:, c, :])], k.dsem("w1s"), [], [w1s.b])
    for c in range(NCH):
        k.dma(k.pool, [(w3s[:, c, :], W3v[:, c, :])], k.dsem("w3s"), [], [w3s.b])
    for j in range(0, NFF, 2):
        k.dma(k.pool, [(w2s[:, j:j + 2, :], W2v[:, j:j + 2, :])], k.dsem("w2s"), [], [w2s.b])

    tiles = []
    if ctx_io is not None:
        tiles.append((ctx_io[0], ctx_io[1], 0, 1))
    for i in range(L // TT):
        tiles.append((xin, xout, i * TT, 0))
    NT = len(tiles)

    def load(i):
        src, _, row0, _ = tiles[i]
        for s in range(2):
            x = xs[i % 2][s]
            k.dma(k.sp, [(x[:, :], src[row0 + s * 128:row0 + (s + 1) * 128, :])], k.dsem(x.b.name), [], [x.b])

    bg = BG()

    def prologue_A(i):
        for s in range(2):
            yield from prologue_a(k, xs[i % 2][s], st6[s], mv[s], rstd[s], half, xn[s])

    def prologue_B(i):
        ms = tiles[i][3]
        for s in range(2):
            prologue_b(k, xn[s], psT[s], ident, hT[i % 2], s * 128, shPs[ms], scPs[ms])

    def mm13(i):
        h = hT[i % 2]
        for j in range(NFF):
            pb = ps13[j % 2]
            for c in range(NCH):
                k.op(k.pe, lambda c=c, j=j, pb=pb: nc.tensor.matmul(pb[:, 0:TT], lhsT=w1s[:, c, j * 128:(j + 1) * 128],
                                                                   rhs=h[:, c, :], start=(c == 0), stop=(c == NCH - 1)),
                     [w1s.b, h.b], [pb.b])
            for c in range(NCH):
                k.op(k.pe, lambda c=c, j=j, pb=pb: nc.tensor.matmul(pb[:, TT:2 * TT], lhsT=w3s[:, c, j * 128:(j + 1) * 128],
                                                                   rhs=h[:, c, :], start=(c == 0), stop=(c == NCH - 1)),
                     [w3s.b, h.b], [pb.b])
            s_ = sg[j % 2]
            k.op(k.act, lambda pb=pb, s_=s_: nc.scalar.activation(out=s_[:, :], in_=pb[:, 0:TT], func=AF.Silu),
                 [pb.b], [s_.b])
            k.op(k.dve, lambda j=j, pb=pb, s_=s_: nc.vector.tensor_tensor(out=u[:, j, :], in0=s_[:, :], in1=pb[:, TT:2 * TT],
                                                                         op=ALU.mult),
                 [pb.b, s_.b], [ubufs[j]])
            bg.step(2)

    def mm2(i):
        for s in range(2):
            for n in range(2):
                py = psy[2 * s + n]
                for j in range(NFF):
                    k.op(k.pe, lambda s=s, n=n, j=j, py=py: nc.tensor.matmul(
                        py[:, :], lhsT=u[:, j, s * 128:(s + 1) * 128], rhs=w2s[:, j, n * 512:(n + 1) * 512],
                        start=(j == 0), stop=(j == NFF - 1)), [ubufs[j], w2s.b], [py.b])

    def epilogue(i):
        _, dst, row0, ms = tiles[i]
        for s in range(2):
            yield from epilogue_sub(k, [psy[2 * s], psy[2 * s + 1]], xs[i % 2][s], gbs[ms], gamb, betb, rr[i % 2][s],
                                    st6[2 + s], mv[2 + s], rstd[2 + s], half,
                                    dst[row0 + s * 128:row0 + (s + 1) * 128, :], rr[i % 2][s].b.name)

    def load_gen(i):
        load(i)
        yield

    load(0)
    if NT > 1:
        load(1)
    run(prologue_A(0))
    prologue_B(0)
    if NT > 1:
        bg.add(prologue_A(1))
    for i in range(NT):
        mm13(i)
        bg.flush()
        if i + 1 < NT:
            prologue_B(i + 1)
        mm2(i)
        bg.add(epilogue(i))
        if i + 2 < NT:
            bg.add(load_gen(i + 2))
            bg.add(prologue_A(i + 2))
    bg.flush()
    k.barrier()
    ph.close()


def phase_even_in(k, io, l, xin):
    nc = k.nc
    ph = Phase(k)
    win = ph.sb("win", [128, NCH, 1920], BF16)
    ident = ph.sb("ident", [128, 128], BF16)
    half = ph.sb("half", [128, 1], F32)
    shPs = [ph.sb(f"shP{r}", [128, NCH], F32) for r in range(2)]
    scPs = [ph.sb(f"scP{r}", [128, NCH], F32) for r in range(2)]
    xs = [ph.sb(f"xs{i}0", [128, D], F32) for i in range(2)]
    xn = [ph.sb(f"xn{i}", [128, D], BF16) for i in range(2)]
    hT = [ph.sb(f"hT{i}", [128, NCH, 128], BF16) for i in range(2)]
    rope = [ph.sb(f"rope{i}", [128, 2, 128], F32) for i in range(2)]
    t1 = [ph.sb(f"t1{i}", [128, 128], F32) for i in range(2)]
    t2 = [ph.sb(f"t2{i}", [128, 128], F32) for i in range(2)]
    qkst = [ph.sb(f"qkst{i}", [128, 5, 128], BF16) for i in range(2)]
    pst = [ph.sb(f"pst{i}", [128, 512], BF16) for i in range(2)]
    vst = [ph.sb(f"vst{i}", [128, 128], BF16) for i in range(2)]
    st6 = [ph.sb(f"st6{i}", [128, 2, 6], F32) for i in range(2)]
    mv = [ph.sb(f"mv{i}", [128, 2], F32) for i in range(2)]
    rstd = [ph.sb(f"rstd{i}", [128, 1], F32) for i in range(2)]
    psT = [ph.ps(f"psT{i}", [128, NCH, 128], BF16) for i in range(2)]
    psP = ph.ps("psP", [128, 512], F32)
    psV = ph.ps("psV", [128, 512], F32)
    psQ = [ph.ps(f"psQ{i}", [128, 512], F32) for i in range(2)]

    k.dma(k.sp, [(ident[:, :], io["ident"])], k.dsem("ident"), [], [ident.b])
    k.op(k.dve, lambda: nc.vector.memset(half[:, :], -0.5), [], [half.b])
    for r_ in range(2):
        load_mod_vectors(k, ph, io["modd"], l, r_, 3, 4, shPs[r_], scPs[r_])
    Wv = io["w_in_all"].rearrange("(c p) n -> p c n", p=128)
    for c in range(NCH):
        k.dma(k.pool, [(win[:, c, :], Wv[:, c, :])], k.dsem("win"), [], [win.b])

    blocks = [(io["ctx1"], 0, 1, 0), (io["ctx1"], 128, 1, 1)] + [(xin, n * 128, 0, n) for n in range(NB)]
    NBK = len(blocks)

    def load(i):
        src, row0, ms, n = blocks[i]
        x = xs[i % 2]
        k.dma(k.sp, [(x[:, :], src[row0:row0 + 128, :])], k.dsem(x.b.name), [], [x.b])
        if ms == 0:
            rp = rope[i % 2]
            k.dma(k.sp, [(rp[:, :, :], io["ropeT"][n])], k.dsem(rp.b.name), [], [rp.b])

    def pa(i):
        run(prologue_a(k, xs[i % 2], st6[i % 2], mv[i % 2], rstd[i % 2], half, xn[i % 2]))

    def pb(i):
        ms = blocks[i][2]
        prologue_b(k, xn[i % 2], psT[i % 2], ident, hT[i % 2], 0, shPs[ms], scPs[ms])

    def body(i):
        src, row0, ms, n = blocks[i]
        h = hT[i % 2]
        qs = qkst[i % 2]
        for c in range(NCH):
            k.op(k.pe, lambda c=c: nc.tensor.matmul(psV[:, 0:128], lhsT=h[:, c, :], rhs=win[:, c, 1792:1920],
                                                    start=(c == 0), stop=(c == NCH - 1)), [h.b, win.b], [psV.b])
        vs = vst[i % 2]
        k.op(k.act, lambda: nc.scalar.copy(out=vs[:, :], in_=psV[:, 0:128]), [psV.b], [vs.b])
        if ms == 1:
            k.dma(k.sp, [(io["vcd"][n], vs[:, :])], k.dsem(vs.b.name), [vs.b], [], store=True)
        else:
            k.dma(k.sp, [(io["vd"][n], vs[:, :])], k.dsem(vs.b.name), [vs.b], [], store=True)
        if ms == 0:
            for c in range(NCH):
                k.op(k.pe, lambda c=c: nc.tensor.matmul(psP[:, :], lhsT=h[:, c, :], rhs=win[:, c, 0:512],
                                                        start=(c == 0), stop=(c == NCH - 1)), [h.b, win.b], [psP.b])
            pt = pst[i % 2]
            k.op(k.act, lambda: nc.scalar.copy(out=pt[:, :], in_=psP[:, :]), [psP.b], [pt.b])
            k.dma(k.sp, [(io["pd"][n], pt[:, :])], k.dsem(pt.b.name), [pt.b], [], store=True)
        prs = range(5) if ms == 0 else [4]
        for j, pr in enumerate(prs):
            pq = psQ[j % 2]
            for c in range(NCH):
                k.op(k.pe, lambda c=c, pr=pr, pq=pq: nc.tensor.matmul(
                    pq[:, 0:128], lhsT=win[:, c, 512 + pr * 128:512 + (pr + 1) * 128], rhs=h[:, c, :],
                    start=(c == 0), stop=(c == NCH - 1)), [h.b, win.b], [pq.b])
            if ms == 0:
                for c in range(NCH):
                    k.op(k.pe, lambda c=c, pr=pr, pq=pq: nc.tensor.matmul(
                        pq[:, 128:256], lhsT=win[:, c, 1152 + pr * 128:1152 + (pr + 1) * 128], rhs=h[:, c, :],
                        start=(c == 0), stop=(c == NCH - 1)), [h.b, win.b], [pq.b])
                rp = rope[i % 2]
                a1 = t1[j % 2]
                a2 = t2[j % 2]
                k.op(k.dve, lambda pq=pq, a1=a1: nc.vector.tensor_tensor(out=a1[:, :], in0=pq[:, 0:128], in1=rp[:, 0, :],
                                                                         op=ALU.mult), [pq.b, rp.b], [a1.b])
                k.op(k.dve, lambda pq=pq, a2=a2: nc.vector.tensor_tensor(out=a2[:, :], in0=pq[:, 128:256], in1=rp[:, 1, :],
                                                                         op=ALU.mult), [pq.b, rp.b], [a2.b])
                k.op(k.pool, lambda pr=pr, a1=a1, a2=a2: nc.gpsimd.tensor_tensor(out=qs[:, pr, :], in0=a1[:, :], in1=a2[:, :],
                                                                                op=ALU.add), [a1.b, a2.b], [qs.b])
            else:
                k.op(k.act, lambda pq=pq: nc.scalar.copy(out=qs[:, 4, :], in_=pq[:, 0:128]), [pq.b], [qs.b])
        if ms == 0:
            k.dma(k.sp, [(io["qkd"][n], qs[:, :, :])], k.dsem(qs.b.name), [qs.b], [], store=True)
        else:
            k.dma(k.sp, [(io["kcd"][:, n * 128:(n + 1) * 128], qs[:, 4, :])], k.dsem(qs.b.name), [qs.b], [], store=True)

    load(0)
    pa(0)
    pb(0)
    for i in range(NBK):
        if i + 1 < NBK:
            load(i + 1)
            pa(i + 1)
        body(i)
        if i + 1 < NBK:
            pb(i + 1)
    k.barrier()
    ph.close()


def phase_even_out(k, io, l, xin, xout):
    nc = k.nc
    AXX = mybir.AxisListType.X
    ph = Phase(k)
    ident = ph.sb("ident", [128, 128], BF16)
    half = ph.sb("half", [128, 1], F32)
    kT = ph.sb("kT", [128, (NB + 2) * 128], BF16)
    vall = ph.sb("vall", [128, NB + 2, 128], BF16)
    pall = ph.sb("pall", [128, NB + 2, 512], BF16)
    kcT = ph.sb("kcT", [128, CTX], BF16)
    vc = ph.sb("vc", [128, 2, 128], BF16)
    masks = ph.sb("masks", [128, 3, 384], BF16)
    band = ph.sb("band", [128, 4, 5, 128], BF16)
    rcnt = ph.sb("rcnt", [128, 2, 4, 128], F32)
    wout = ph.sb("wout", [128, NCH, D], BF16)
    poolw = ph.sb("poolw", [128, 4, 128], BF16)
    pscale = ph.sb("pscale", [128, 4], F32)
    sinkc = ph.sb("sinkc", [128, 8], F32)
    nsink = ph.sb("nsink", [128, 8], F32)
    gb = ph.sb("gb0", [128, D], F32)
    gamb = ph.sb("gamb", [128, D], F32)
    betb = ph.sb("betb", [128, D], F32)
    qb = [ph.sb(f"qb{i}", [128, 4, 128], BF16) for i in range(2)]
    xs = [ph.sb(f"xs{i}0", [128, D], F32) for i in range(2)]
    rr = [ph.sb(f"r{i}0", [128, D], F32) for i in range(2)]
    dT = ph.sb("dT", [128, 4, 128], BF16)
    mixT = ph.sb("mixT", [128, NCH, 128], BF16)
    P = [ph.sb(f"P{i}", [128, 640], BF16) for i in range(2)]
    PT = [ph.sb(f"PT{i}", [128, 5, 128], BF16) for i in range(2)]
    mx = [ph.sb(f"mx{h}", [128, 1], F32) for h in range(8)]
    negm = [ph.sb(f"negm{h}", [128, 1], F32) for h in range(8)]
    rs = [ph.sb(f"rs{h}", [128, 1], F32) for h in range(8)]
    es = [ph.sb(f"es{h}", [128, 1], F32) for h in range(8)]
    den = ph.sb("den", [128, 8], F32)
    rden = ph.sb("rden", [128, 8], F32)
    osb = ph.sb("osb", [128, 512], BF16)
    st6 = ph.sb("st60", [128, 2, 6], F32)
    mv = ph.sb("mv0", [128, 2], F32)
    rstd = ph.sb("rstd0", [128, 1], F32)
    S = [ph.ps(f"S{i}", [128, 1024], F32) for i in range(2)]
    PTp = ph.ps("PTp", [128, 5, 128], BF16)
    Ops = ph.ps("Ops", [128, 512], F32)
    psyx = ph.ps("psyx", [128, 1024], F32)
    psy0 = T(psyx.t, "psyx"); psy0.b = psyx.b

    class _V:
        def __init__(self, t, lo, hi, b):
            self.t, self.lo, self.hi, self.b = t, lo, hi, b

        def __getitem__(self, idx):
            return self.t[:, self.lo:self.hi]
    psyh = [_V(psyx.t, 0, 512, psyx.b), _V(psyx.t, 512, 1024, psyx.b)]

    sp = k.sp
    k.dma(sp, [(ident[:, :], io["ident"])], k.dsem("ident"), [], [ident.b])
    k.op(k.dve, lambda: nc.vector.memset(half[:, :], -0.5), [], [half.b])
    k.dma(sp, [(masks[:, :, :], io["masks"])], k.dsem("masks"), [], [masks.b])
    k.dma(sp, [(band[:, :, :, :], io["band"])], k.dsem("band"), [], [band.b])
    k.dma(sp, [(rcnt[:, :, :, :], io["rcnt"].broadcast_to([128, 2, 4, 128]))], k.dsem("rcnt"), [], [rcnt.b])
    k.dma(sp, [(sinkc[:, :], io["attn_sinks"].broadcast_to([128, 8]))], k.dsem("sinkc"), [], [sinkc.b])
    k.op(k.dve, lambda: nc.vector.tensor_scalar_mul(out=nsink[:, :], in0=sinkc[:, :], scalar1=-1.0), [sinkc.b], [nsink.b])
    k.dma(sp, [(pscale[:, :], io["pool_scale"].rearrange("(g p) -> p g", p=128))], k.dsem("pscale"), [], [pscale.b],
          allow_slow_non_contiguous=True)
    k.dma(k.pool, [(poolw[:, :, :], io["pool_w"].rearrange("g c d -> c g d"))], k.dsem("poolw"), [], [poolw.b])
    Wv = io["mix_w_out"].rearrange("(c p) n -> p c n", p=128)
    for c in range(0, NCH, 2):
        k.dma(k.pool, [(wout[:, c:c + 2, :], Wv[:, c:c + 2, :])], k.dsem("wout"), [], [wout.b])
    load_bcast(k, gb, io["modd"][l, 0:1, 5 * D:6 * D], 1.0 / ALPHA)
    load_bcast(k, gamb, io["ln_g"][l, 1:2, :])
    load_bcast(k, betb, io["ln_b"][l, 1:2, :])
    k.op(k.dve, lambda: nc.vector.memset(kT[:, 0:128], 0.0), [], [kT.b])
    k.op(k.dve, lambda: nc.vector.memset(kT[:, (NB + 1) * 128:(NB + 2) * 128], 0.0), [], [kT.b])
    k.op(k.dve, lambda: nc.vector.memset(vall[:, 0, :], 0.0), [], [vall.b])
    k.op(k.dve, lambda: nc.vector.memset(vall[:, NB + 1, :], 0.0), [], [vall.b])
    k.op(k.dve, lambda: nc.vector.memset(pall[:, 0, :], 0.0), [], [pall.b])
    k.op(k.dve, lambda: nc.vector.memset(pall[:, NB + 1, :], 0.0), [], [pall.b])
    kTv = kT.t[:, 128:(NB + 1) * 128].rearrange("p (n t) -> p n t", t=128)
    prs = []
    for n0 in range(0, NB, 16):
        prs.append((kTv[:, n0:n0 + 16, :], io["qkd"][n0:n0 + 16, :, 4, :].rearrange("n p t -> p n t")))
    k.dma(sp, prs, k.dsem("kT"), [], [kT.b])
    prs = []
    for n0 in range(0, NB, 16):
        prs.append((vall[:, 1 + n0:1 + n0 + 16, :], io["vd"][n0:n0 + 16].rearrange("n p d -> p n d")))
    k.dma(sp, prs, k.dsem("vall"), [], [vall.b])
    prs = []
    for n0 in range(0, NB, 8):
        prs.append((pall[:, 1 + n0:1 + n0 + 8, :], io["pd"][n0:n0 + 8].rearrange("n p d -> p n d")))
    k.dma(sp, prs, k.dsem("pall"), [], [pall.b])
    k.dma(sp, [(kcT[:, :], io["kcd"])], k.dsem("kcT"), [], [kcT.b])
    k.dma(sp, [(vc[:, :, :], io["vcd"].rearrange("n p d -> p n d"))], k.dsem("vc"), [], [vc.b])

    bg = BG()

    def loadq(n):
        q = qb[n % 2]
        k.dma(sp, [(q[:, :, :], io["qkd"][n, :, 0:4, :])], k.dsem(q.b.name), [], [q.b])

    def loadx(n):
        x = xs[n % 2]
        k.dma(sp, [(x[:, :], xin[n * 128:(n + 1) * 128, :])], k.dsem(x.b.name), [], [x.b])

    def scores(n, h):
        i, hh = h % 4, h // 4
        b0 = hh * 64
        q = qb[n % 2]
        Sb = S[h % 2]
        mt = 0 if n == 0 else (2 if n == NB - 1 else 1)
        k.op(k.pe, lambda: nc.tensor.matmul(Sb[:, 256:512], lhsT=q[b0:b0 + 64, i, :], rhs=kcT[b0:b0 + 64, :],
                                            start=True, stop=True), [q.b, kcT.b], [Sb.b])
        k.op(k.pe, lambda: nc.tensor.matmul(Sb[:, 512:896], lhsT=q[b0:b0 + 64, i, :], rhs=kT[b0:b0 + 64, n * 128:n * 128 + 384],
                                            start=True, stop=False), [q.b, kT.b], [Sb.b])
        k.op(k.pe, lambda: nc.tensor.matmul(Sb[:, 512:896], lhsT=ident[:, :], rhs=masks[:, mt, :],
                                            start=False, stop=True), [ident.b, masks.b], [Sb.b])

    def softmax(n, h):
        Sb = S[h % 2]
        Pb = P[h % 2]
        k.op(k.dve, lambda: nc.vector.reduce_max(out=mx[h][:, :], in_=Sb[:, 256:896], axis=AXX), [Sb.b], [mx[h].b])
        k.op(k.dve, lambda: nc.vector.tensor_scalar(out=negm[h][:, :], in0=mx[h][:, :], scalar1=-0.125, scalar2=nsink[:, h:h + 1],
                                                    op0=ALU.mult, op1=ALU.min), [mx[h].b, nsink.b], [negm[h].b])
        k.op(k.act, lambda: nc.scalar.activation(out=Pb[:, :], in_=Sb[:, 256:896], func=AF.Exp, bias=negm[h][:, 0:1],
                                                 scale=0.125, accum_out=rs[h][:, 0:1]), [Sb.b, negm[h].b], [Pb.b, rs[h].b])
        k.op(k.act, lambda: nc.scalar.activation(out=es[h][:, :], in_=negm[h][:, :], func=AF.Exp, bias=sinkc[:, h:h + 1],
                                                 scale=1.0), [negm[h].b, sinkc.b], [es[h].b])
        k.op(k.dve, lambda: nc.vector.tensor_tensor(out=den[:, h:h + 1], in0=rs[h][:, :], in1=es[h][:, :], op=ALU.add),
             [rs[h].b, es[h].b], [den.b])

    def pv(n, h):
        kvh = h // 4
        Pb = P[h % 2]
        PTb = PT[h % 2]
        for kb in range(5):
            k.op(k.pe, lambda kb=kb: nc.tensor.transpose(out=PTp[:, kb, :], in_=Pb[:, kb * 128:(kb + 1) * 128],
                                                         identity=ident[:, :]), [Pb.b, ident.b], [PTp.b])
        k.op(k.dve, lambda: nc.vector.tensor_copy(out=PTb[:, :, :], in_=PTp[:, :, :]), [PTp.b], [PTb.b])
        for kb in range(5):
            if kb < 2:
                rhs = vc[:, kb, kvh * 64:(kvh + 1) * 64]
                rb = vc.b
            else:
                rhs = vall[:, n + kb - 2, kvh * 64:(kvh + 1) * 64]
                rb = vall.b
            k.op(k.pe, lambda kb=kb, rhs=rhs: nc.tensor.matmul(Ops[:, h * 64:(h + 1) * 64], lhsT=PTb[:, kb, :], rhs=rhs,
                                                               start=(kb == 0), stop=(kb == 4)), [PTb.b, rb], [Ops.b])

    def pooling(n):
        for g in range(4):
            for rel in range(3):
                var = rel
                if rel == 1 and n == 0:
                    var = 3
                if rel == 1 and n == NB - 1:
                    var = 4
                k.op(k.pe, lambda g=g, rel=rel, var=var: nc.tensor.matmul(
                    psyx[:, g * 128:(g + 1) * 128], lhsT=pall[:, n + rel, g * 128:(g + 1) * 128], rhs=band[:, g, var, :],
                    start=(rel == 0), stop=(rel == 2)), [pall.b, band.b], [psyx.b])
        for g in range(4):
            w = (2, 4, 8, 16)[g]
            if n == 0 or n == NB - 1:
                e = 0 if n == 0 else 1
                k.op(k.dve, lambda g=g, e=e: nc.vector.tensor_tensor(out=dT[:, g, :], in0=psyx[:, g * 128:(g + 1) * 128],
                                                                    in1=rcnt[:, e, g, :], op=ALU.mult),
                     [psyx.b, rcnt.b], [dT.b])
            else:
                k.op(k.dve, lambda g=g, w=w: nc.vector.tensor_scalar_mul(out=dT[:, g, :], in0=psyx[:, g * 128:(g + 1) * 128],
                                                                        scalar1=1.0 / w), [psyx.b], [dT.b])
        for g in range(4):
            k.op(k.pe, lambda g=g: nc.tensor.matmul(psyx[:, 512 + g * 128:512 + (g + 1) * 128], lhsT=poolw[:, g, :],
                                                    rhs=dT[:, g, :], start=True, stop=True), [poolw.b, dT.b], [psyx.b])
        for g in range(4):
            k.op(k.act, lambda g=g: nc.scalar.activation(out=mixT[:, g, :], in_=psyx[:, 512 + g * 128:512 + (g + 1) * 128],
                                                         func=AF.Copy, scale=pscale[:, g:g + 1]), [psyx.b, pscale.b], [mixT.b])

    def finish(n):
        k.op(k.dve, lambda: nc.vector.reciprocal(out=rden[:, :], in_=den[:, :]), [den.b], [rden.b])
        k.op(k.dve, lambda: nc.vector.tensor_tensor(
            out=osb[:, :].rearrange("p (h d) -> p h d", d=64), in0=Ops[:, :].rearrange("p (h d) -> p h d", d=64),
            in1=rden[:, :].unsqueeze(2).broadcast_to([128, 8, 64]), op=ALU.mult), [Ops.b, rden.b], [osb.b])
        Sv = S[0]
        tp = Sv.t[:, 0:256].bitcast(BF16).rearrange("p (c t) -> p c t", t=128)
        for c in range(4):
            k.op(k.pe, lambda c=c: nc.tensor.transpose(out=tp[:, c, :], in_=osb[:, c * 128:(c + 1) * 128], identity=ident[:, :]),
                 [osb.b, ident.b], [Sv.b])
        k.op(k.act, lambda: nc.scalar.copy(out=mixT[:, 4:8, :], in_=tp[:, :, :]), [Sv.b], [mixT.b])
        for nh in range(2):
            for c in range(NCH):
                k.op(k.pe, lambda c=c, nh=nh: nc.tensor.matmul(psyx[:, nh * 512:(nh + 1) * 512], lhsT=mixT[:, c, :],
                                                               rhs=wout[:, c, nh * 512:(nh + 1) * 512],
                                                               start=(c == 0), stop=(c == NCH - 1)), [mixT.b, wout.b], [psyx.b])

    def epi(n):
        yield from epilogue_sub(k, psyh, xs[n % 2], gb, gamb, betb, rr[n % 2], st6, mv, rstd, half,
                                xout[n * 128:(n + 1) * 128, :], rr[n % 2].b.name)

    loadq(0)
    loadx(0)
    loadx(1)
    for n in range(NB):
        if n + 1 < NB:
            loadq(n + 1)
        scores(n, 0)
        for h in range(8):
            softmax(n, h)
            if h + 1 < 8:
                scores(n, h + 1)
            pv(n, h)
            bg.step(2)
        bg.flush()
        if n >= 1 and n + 1 < NB:
            loadx(n + 1)
        pooling(n)
        finish(n)
        bg.add(epi(n))
    bg.flush()
    k.barrier()
    ph.close()


def phase_odd_in(k, io, l, xin):
    nc = k.nc
    ph = Phase(k)
    ident = ph.sb("ident", [128, 128], BF16)
    half = ph.sb("half", [128, 1], F32)
    shP = ph.sb("shP0", [128, NCH], F32)
    scP = ph.sb("scP0", [128, NCH], F32)
    ttab = ph.sb("ttab", [128, 64, 3, 128], BF16)
    cs256 = ph.sb("cs256", [128, 2, 512], BF16)
    xs = [ph.sb(f"xs{i}0", [128, D], F32) for i in range(2)]
    xn = [ph.sb(f"xn{i}", [128, D], BF16) for i in range(2)]
    hT = [ph.sb(f"hT{i}", [128, NCH, 128], BF16) for i in range(2)]
    ABs = [ph.sb(f"ABs{i}", [128, 2, 4, 256], BF16) for i in range(2)]
    Yst = [ph.sb(f"Yst{i}", [128, 2, D], BF16) for i in range(2)]
    st6 = [ph.sb(f"st6{i}", [128, 2, 6], F32) for i in range(2)]
    mv = [ph.sb(f"mv{i}", [128, 2], F32) for i in range(2)]
    rstd = [ph.sb(f"rstd{i}", [128, 1], F32) for i in range(2)]
    psT = [ph.ps(f"psT{i}", [128, NCH, 128], BF16) for i in range(2)]
    psAB = [ph.ps(f"psAB{i}", [128, 512], F32) for i in range(2)]
    psY = [ph.ps(f"psY{i}", [128, 512], F32) for i in range(2)]
    sp = k.sp
    k.dma(sp, [(ident[:, :], io["ident"])], k.dsem("ident"), [], [ident.b])
    k.op(k.dve, lambda: nc.vector.memset(half[:, :], -0.5), [], [half.b])
    load_mod_vectors(k, ph, io["modd"], l, 0, 3, 4, shP, scP)
    k.dma(sp, [(ttab[:, 0:32, :, :], io["ttab"][:, 0:32, :, :]), (ttab[:, 32:64, :, :], io["ttab"][:, 32:64, :, :])],
          k.dsem("ttab"), [], [ttab.b])
    k.dma(sp, [(cs256[:, :, :], io["cs256"].rearrange("(kc p) n -> p kc n", p=128))], k.dsem("cs256"), [], [cs256.b])
    xv = xin.rearrange("(p j) d -> j p d", j=64)
    ydv = io["yd"]

    def load(j):
        x = xs[j % 2]
        k.dma(sp, [(x[:, :], xv[j])], k.dsem(x.b.name), [], [x.b])

    def pa(j):
        run(prologue_a(k, xs[j % 2], st6[j % 2], mv[j % 2], rstd[j % 2], half, xn[j % 2]))

    def pb(j):
        prologue_b(k, xn[j % 2], psT[j % 2], ident, hT[j % 2], 0, shP, scP)

    def body(j):
        h = hT[j % 2]
        ab = ABs[j % 2]
        for g in range(4):
            pab = psAB[g % 2]
            for kc in range(2):
                k.op(k.pe, lambda g=g, kc=kc, pab=pab: nc.tensor.matmul(pab[:, :], lhsT=h[:, 2 * g + kc, :], rhs=cs256[:, kc, :],
                                                                       start=(kc == 0), stop=(kc == 1)), [h.b, cs256.b], [pab.b])
            eng = k.act if g % 2 == 0 else k.dve
            if g % 2 == 0:
                k.op(k.act, lambda g=g, pab=pab: nc.scalar.copy(out=ab[:, :, g, :], in_=pab[:, :].rearrange("p (r c) -> p r c", r=2)),
                     [pab.b], [ab.b])
            else:
                k.op(k.dve, lambda g=g, pab=pab: nc.vector.tensor_copy(out=ab[:, :, g, :], in_=pab[:, :].rearrange("p (r c) -> p r c", r=2)),
                     [pab.b], [ab.b])
        ys = Yst[j % 2]
        A = ab.t[:, 0, :, :].rearrange("p g c -> p (g c)")
        B = ab.t[:, 1, :, :].rearrange("p g c -> p (g c)")
        it = 0
        for ri in range(2):
            for ch in range(2):
                py = psY[it % 2]
                it += 1
                cs_ = slice(ch * 512, (ch + 1) * 512)
                ta, tb = (0, 2) if ri == 0 else (1, 0)
                k.op(k.pe, lambda py=py, ta=ta, cs_=cs_: nc.tensor.matmul(py[:, :], lhsT=ttab[:, j, ta, :], rhs=A[:, cs_],
                                                                         start=True, stop=False), [ttab.b, ab.b], [py.b])
                k.op(k.pe, lambda py=py, tb=tb, cs_=cs_: nc.tensor.matmul(py[:, :], lhsT=ttab[:, j, tb, :], rhs=B[:, cs_],
                                                                         start=False, stop=True), [ttab.b, ab.b], [py.b])
                if it % 2 == 0:
                    k.op(k.act, lambda py=py, ri=ri, cs_=cs_: nc.scalar.copy(out=ys[:, ri, cs_], in_=py[:, :]), [py.b], [ys.b])
                else:
                    k.op(k.dve, lambda py=py, ri=ri, cs_=cs_: nc.vector.tensor_copy(out=ys[:, ri, cs_], in_=py[:, :]), [py.b], [ys.b])
        k.dma(sp, [(ydv[0, j], ys[:, 0, :]), (ydv[1, j], ys[:, 1, :])], k.dsem(ys.b.name), [ys.b], [], store=True)

    load(0)
    pa(0)
    pb(0)
    for j in range(64):
        if j + 1 < 64:
            load(j + 1)
            pa(j + 1)
        body(j)
        if j + 1 < 64:
            pb(j + 1)
    k.barrier()
    ph.close()


def phase_odd_out(k, io, l, xin, xout):
    nc = k.nc
    ph = Phase(k)
    half = ph.sb("half", [128, 1], F32)
    w2stk = ph.sb("w2stk", [128, 64], BF16)
    fw = ph.sb("wout", [128, NCH, D], BF16)
    gb = ph.sb("gb0", [128, D], F32)
    gamb = ph.sb("gamb", [128, D], F32)
    betb = ph.sb("betb", [128, D], F32)
    Ystk = [ph.sb(f"Ystk{i}", [128, 2, D], BF16) for i in range(2)]
    fT = [ph.sb(f"fT{i}", [128, NCH, 128], BF16) for i in range(2)]
    xs = [ph.sb(f"xs{i}0", [128, D], F32) for i in range(2)]
    rr = [ph.sb(f"r{i}0", [128, D], F32) for i in range(2)]
    st6 = ph.sb("st60", [128, 2, 6], F32)
    mv = ph.sb("mv0", [128, 2], F32)
    rstd = ph.sb("rstd0", [128, 1], F32)
    psF = [ph.ps(f"psF{i}", [128, NCH, 128], F32) for i in range(2)]
    psy = [ph.ps(f"psy{i}", [128, 512], F32) for i in range(2)]
    sp = k.sp
    k.op(k.dve, lambda: nc.vector.memset(half[:, :], -0.5), [], [half.b])
    k.dma(sp, [(w2stk[:, :], io["w2stk"])], k.dsem("w2stk"), [], [w2stk.b])
    Wv = io["fourier_w_out"].rearrange("(c p) n -> p c n", p=128)
    for c in range(0, NCH, 2):
        k.dma(k.pool, [(fw[:, c:c + 2, :], Wv[:, c:c + 2, :])], k.dsem("wout"), [], [fw.b])
    load_bcast(k, gb, io["modd"][l, 0:1, 5 * D:6 * D], 1.0 / ALPHA)
    load_bcast(k, gamb, io["ln_g"][l, 1:2, :])
    load_bcast(k, betb, io["ln_b"][l, 1:2, :])
    xv = xin.rearrange("(k2 k1) d -> k1 k2 d", k1=128)
    ov = xout.rearrange("(k2 k1) d -> k1 k2 d", k1=128)
    ydv = io["yd"]
    fscale = float(1.0 / np.sqrt(L * 256.0))
    bg = BG()

    def loady(m):
        y = Ystk[m % 2]
        k.dma(sp, [(y[0:64, :, :], ydv[0, :, 2 * m:2 * m + 2, :]), (y[64:128, :, :], ydv[1, :, 2 * m:2 * m + 2, :])],
              k.dsem(y.b.name), [], [y.b])

    def loadx(m):
        x = xs[m % 2]
        k.dma(sp, [(x[0:64, :], xv[2 * m]), (x[64:128, :], xv[2 * m + 1])], k.dsem(x.b.name), [], [x.b])

    def body(m):
        y = Ystk[m % 2]
        pf = psF[m % 2]
        f = fT[m % 2]
        for par in range(2):
            for c in range(NCH):
                k.op(k.pe, lambda par=par, c=c: nc.tensor.matmul(pf[:, c, par * 64:(par + 1) * 64],
                                                                 lhsT=y[:, par, c * 128:(c + 1) * 128], rhs=w2stk[:, :],
                                                                 start=True, stop=True), [y.b, w2stk.b], [pf.b])
        k.op(k.act, lambda: nc.scalar.mul(out=f[:, 0:4, :], in_=pf[:, 0:4, :], mul=fscale), [pf.b], [f.b])
        k.op(k.dve, lambda: nc.vector.tensor_scalar_mul(out=f[:, 4:8, :], in0=pf[:, 4:8, :], scalar1=fscale), [pf.b], [f.b])
        for nh in range(2):
            for c in range(NCH):
                k.op(k.pe, lambda c=c, nh=nh: nc.tensor.matmul(psy[nh][:, :], lhsT=f[:, c, :], rhs=fw[:, c, nh * 512:(nh + 1) * 512],
                                                               start=(c == 0), stop=(c == NCH - 1)), [f.b, fw.b], [psy[nh].b])

    class _O:
        pass

    def epi(m):
        r = rr[m % 2]
        nc_ = nc
        gen = epilogue_sub(k, psy, xs[m % 2], gb, gamb, betb, r, st6, mv, rstd, half, None, r.b.name, store_pairs=[
            (ov[2 * m], r[0:64, :]), (ov[2 * m + 1], r[64:128, :])])
        yield from gen

    loady(0)
    loadx(0)
    loadx(1)
    for m in range(64):
        if m + 1 < 64:
            loady(m + 1)
        body(m)
        bg.flush()
        bg.add(epi(m))
        if m >= 1 and m + 1 < 64:
            pass
        bg.flush()
        if m + 2 < 64:
            loadx(m + 2)
    k.barrier()
    ph.close()


def build(nphases=99, dbg=()):
    nc = bass.Bass("TRN2", target_bir_lowering=False)
    io = {}

    def inp(name, shape, dt=F32):
        io[name] = nc.dram_tensor(name, list(shape), dt, kind="ExternalInput").ap()

    def scr(name, shape, dt=F32):
        io[name] = nc.dram_tensor(name, list(shape), dt, kind="ExternalOutput" if name in dbg else "Internal").ap()

    inp("x", [L, D]); inp("ctx", [CTX, D]); inp("cvecT", [128, NCH, 2])
    inp("ada_w", [2, D, 9 * D]); inp("ada_b", [2, 9 * D])
    inp("ln_g", [2, 3, D]); inp("ln_b", [2, 3, D])
    for nm in ("ffn1_w1", "ffn1_w3", "ffn2_w1", "ffn2_w3"):
        inp(nm, [2, D, FF])
    for nm in ("ffn1_w2", "ffn2_w2"):
        inp(nm, [2, FF, D])
    inp("w_in_all", [D, 1920]); inp("pool_w", [4, 128, 128]); inp("pool_scale", [512]); inp("attn_sinks", [1, 8])
    inp("mix_w_out", [D, D]); inp("fourier_w_out", [D, D])
    inp("ident", [128, 128], BF16)
    inp("ropeT", [NB, 128, 2, 128]); inp("masks", [128, 3, 384], BF16); inp("band", [128, 4, 5, 128], BF16)
    inp("rcnt", [1, 2, 4, 128]); inp("ttab", [128, 64, 3, 128], BF16); inp("cs256", [256, 512], BF16)
    inp("w2stk", [128, 64], BF16)
    io["out"] = nc.dram_tensor("out", [L, D], F32, kind="ExternalOutput").ap()
    scr("modd", [2, 2, 9 * D])
    scr("xa", [L, D]); scr("xb", [L, D]); scr("xc", [L, D]); scr("ctx1", [CTX, D])
    scr("pd", [NB, 128, 512], BF16); scr("qkd", [NB, 128, 5, 128], BF16); scr("vd", [NB, 128, 128], BF16)
    scr("kcd", [128, CTX], BF16); scr("vcd", [2, 128, 128], BF16)
    scr("yd", [2, 64, 128, D], BF16)

    k = Kern(nc)
    F = lambda nm, l: io[nm][l]
    phases = [
        lambda dst: phase_mod(k, io),
        lambda dst: phase_ffn(k, io, 0, F("ffn1_w1", 0), F("ffn1_w3", 0), F("ffn1_w2", 0), 0, 1, 2, 0,
                              io["x"], dst or io["xa"], (io["ctx"], io["ctx1"])),
        lambda dst: phase_even_in(k, io, 0, io["xa"]),
        lambda dst: phase_even_out(k, io, 0, io["xa"], dst or io["xb"]),
        lambda dst: phase_ffn(k, io, 0, F("ffn2_w1", 0), F("ffn2_w3", 0), F("ffn2_w2", 0), 6, 7, 8, 2,
                              io["xb"], dst or io["xc"]),
        lambda dst: phase_ffn(k, io, 1, F("ffn1_w1", 1), F("ffn1_w3", 1), F("ffn1_w2", 1), 0, 1, 2, 0,
                              io["xc"], dst or io["xa"]),
        lambda dst: phase_odd_in(k, io, 1, io["xa"]),
        lambda dst: phase_odd_out(k, io, 1, io["xa"], dst or io["xb"]),
        lambda dst: phase_ffn(k, io, 1, F("ffn2_w1", 1), F("ffn2_w3", 1), F("ffn2_w2", 1), 6, 7, 8, 2,
                              io["xb"], dst or io["out"]),
    ]
    n = min(nphases, len(phases))
    for i in range(n):
        phases[i](io["out"] if i == n - 1 and i > 0 else None)
    return nc


def make_consts():
    c = {}
    c["ident"] = np.eye(128, dtype=np.float32).astype(NPBF)
    t = np.arange(L)
    row = (t // 64).astype(np.float64)
    col = (t % 64).astype(np.float64)
    freqs = (10000.0 ** (-np.arange(0, 32, 2, dtype=np.float32) / 32.0)).astype(np.float32).astype(np.float64)
    d = np.arange(64)
    dd = d % 32
    fi = dd % 16
    pos = np.where(d[:, None] < 32, row[None, :], col[None, :])
    ang = (pos.astype(np.float32) * freqs[fi][:, None].astype(np.float32)).astype(np.float64)
    sign = np.where(dd < 16, -1.0, 1.0)[:, None]
    cosT = np.cos(ang)
    sinT = np.sin(ang) * sign
    tab = np.stack([cosT, sinT], axis=1)
    tab = np.concatenate([tab, tab], axis=0)
    c["ropeT"] = np.ascontiguousarray(tab.reshape(128, 2, NB, 128).transpose(2, 0, 1, 3)).astype(np.float32)
    qi = np.arange(128)[:, None]
    ki = np.arange(384)[None, :]
    base = np.abs(ki - 128 - qi) <= 128
    m = np.zeros((128, 3, 384), np.float32)
    m[:, 0] = np.where(base & (ki >= 128), 0.0, NEG)
    m[:, 1] = np.where(base, 0.0, NEG)
    m[:, 2] = np.where(base & (ki < 256), 0.0, NEG)
    c["masks"] = m.astype(NPBF)
    band = np.zeros((128, 4, 5, 128), np.float32)
    rc = np.zeros((1, 2, 4, 128), np.float32)
    src = np.arange(128)[:, None]
    dst = np.arange(128)[None, :]
    for g, w in enumerate((2, 4, 8, 16)):
        hw = w // 2
        inwin = lambda s_glob: ((s_glob >= dst - hw) & (s_glob <= dst + hw - 1)).astype(np.float32)
        eye = (src == dst).astype(np.float32)
        band[:, g, 0] = inwin(src - 128)
        band[:, g, 1] = inwin(src) - w * eye
        band[:, g, 2] = inwin(src + 128)
        cnt_first = np.minimum(dst + hw, L) - np.maximum(dst - hw, 0)
        tg = (L - 128) + dst
        cnt_last = np.minimum(tg + hw, L) - np.maximum(tg - hw, 0)
        band[:, g, 3] = inwin(src) - cnt_first * eye
        band[:, g, 4] = inwin(src) - cnt_last * eye
        rc[0, 0, g] = 1.0 / cnt_first[0]
        rc[0, 1, g] = 1.0 / cnt_last[0]
    c["band"] = band.astype(NPBF)
    c["rcnt"] = rc
    n1 = np.arange(128)[:, None, None]
    j = np.arange(64)[None, :, None]
    k1 = np.arange(128)[None, None, :]
    th = 2.0 * np.pi * (((64 * n1 + j) * k1) % L) / L
    c["ttab"] = np.stack([np.cos(th), np.sin(th), -np.sin(th)], axis=2).astype(np.float32).astype(NPBF)
    cc = np.arange(256)
    th2 = 2.0 * np.pi * ((cc[:, None] * cc[None, :]) % 256) / 256.0
    c["cs256"] = np.concatenate([np.cos(th2), np.sin(th2)], axis=1).astype(np.float32).astype(NPBF)
    jj = np.arange(64)
    th3 = 2.0 * np.pi * ((jj[:, None] * jj[None, :]) % 64) / 64.0
    c["w2stk"] = np.concatenate([np.cos(th3), -np.sin(th3)], axis=0).astype(np.float32).astype(NPBF)
    return c


def layout_w_in(w_in):
    w_in = np.asarray(w_in, dtype=np.float32)
    p = w_in[:, 0:512]
    q = w_in[:, 512:1024].reshape(D, 8, 64)
    kk = w_in[:, 1024:1152].reshape(D, 2, 64)
    v = w_in[:, 1152:1280]
    pairs = [np.concatenate([q[:, i], q[:, i + 4]], axis=1) for i in range(4)] + [np.concatenate([kk[:, 0], kk[:, 1]], axis=1)]
    qk = np.concatenate(pairs, axis=1)
    d = np.arange(64)
    partner = np.where((d % 32) < 16, d + 16, d - 16)
    idx = np.concatenate([blk * 64 + partner for blk in range(10)])
    qksw = qk[:, idx]
    return np.ascontiguousarray(np.concatenate([p, qk, qksw, v], axis=1))


def make_in_maps(inputs, cores=range(8)):
    consts = make_consts()
    f32 = lambda a: np.ascontiguousarray(np.asarray(a, dtype=np.float32))
    shared = {}
    for nm in ("ada_w", "ada_b", "ln_g", "ln_b", "ffn1_w1", "ffn1_w3", "ffn1_w2", "ffn2_w1", "ffn2_w3", "ffn2_w2"):
        shared[nm] = f32(inputs[nm])
    shared["w_in_all"] = layout_w_in(inputs["mix_w_in"][0])
    shared["pool_w"] = f32(inputs["pool_w"][0])
    shared["pool_scale"] = f32(inputs["pool_scale"][0])
    shared["attn_sinks"] = f32(inputs["attn_sinks"][0]).reshape(1, 8)
    shared["mix_w_out"] = f32(inputs["mix_w_out"][0])
    shared["fourier_w_out"] = f32(inputs["fourier_w_out"][0])
    shared.update(consts)
    x = inputs["x"]; ctx = inputs["ctx"]; c = f32(inputs["c"]); cc = f32(inputs["c_ctx"])
    in_maps = []
    for b in cores:
        m = dict(shared)
        m["x"] = f32(x[b])
        m["ctx"] = f32(ctx[b])
        cv = np.stack([c[b], cc], axis=0)
        m["cvecT"] = np.ascontiguousarray(cv.reshape(2, NCH, 128).transpose(2, 1, 0))
        in_maps.append(m)
    return in_maps


def kernel(**inputs):
    nc = build()
    in_maps = make_in_maps(inputs)
    res = run_bass_kernel_spmd(nc, in_maps, core_ids=list(range(8)))
    return np.stack([r["out"] for r in res.results], axis=0)
```

```python
import contextlib
import numpy as np
import ml_dtypes
import concourse.bass as bass
import concourse.mybir as mybir
from concourse.bass_utils import run_bass_kernel_spmd

F32 = mybir.dt.float32
BF16 = mybir.dt.bfloat16
AF = mybir.ActivationFunctionType
ALU = mybir.AluOpType
NPBF = ml_dtypes.bfloat16

D = 1024
L = 8192
FF = 2816
NFF = 22
NCH = 8
CTX = 256
NB = 64
ALPHA = 4.0 ** 0.25
EPS = 1e-5
EPS_A = EPS / (ALPHA * ALPHA)
NEG = -30000.0


class Eng:
    def __init__(self, nc, name, eng, is_pe=False):
        self.name = name
        self.e = eng
        self.is_pe = is_pe
        self.sem = nc.alloc_semaphore(name="es_" + name)
        self.cnt = 0
        self.waited = {}

    def wait(self, ev):
        key, sem, val, _ = ev
        if self.waited.get(key, 0) >= val:
            return
        self.e.wait_ge(sem, val)
        self.waited[key] = val


class Buf:
    def __init__(self, name):
        self.name = name
        self.w = None
        self.r = {}


class DSem:
    def __init__(self, nc, name):
        self.name = name
        self.sem = nc.alloc_semaphore(name="ds_" + name)
        self.cnt = 0


class Kern:
    def __init__(self, nc):
        self.nc = nc
        self.pe = Eng(nc, "pe", nc.tensor, is_pe=True)
        self.act = Eng(nc, "act", nc.scalar)
        self.dve = Eng(nc, "dve", nc.vector)
        self.pool = Eng(nc, "pool", nc.gpsimd)
        self.sp = Eng(nc, "sp", nc.sync)
        self.engs = [self.pe, self.act, self.dve, self.pool, self.sp]
        self.stores = {}
        self.dsems = {}

    def dsem(self, name):
        if name not in self.dsems:
            self.dsems[name] = DSem(self.nc, name)
        return self.dsems[name]

    def _deps(self, eng, reads, writes):
        for b in reads:
            if b.w is not None:
                self._need(eng, b.w, True)
        for b in writes:
            if b.w is not None:
                self._need(eng, b.w, False)
            for ev in b.r.values():
                self._need(eng, ev, False)

    def _need(self, eng, ev, raw):
        src = ev[3]
        if src is eng:
            if eng.is_pe or not raw:
                return
        eng.wait(ev)

    def op(self, eng, fn, reads=(), writes=()):
        self._deps(eng, reads, writes)
        ins = fn()
        eng.cnt += 1
        ins.then_inc(eng.sem, 1)
        ev = (eng.name, eng.sem, eng.cnt, eng)
        for b in reads:
            b.r[eng.name] = ev
        for b in writes:
            b.w = ev
            b.r = {}
        return ev

    def dma(self, q, pairs, ds, reads=(), writes=(), store=False, **kw):
        self._deps(q, reads, writes)
        for (o, i) in pairs:
            ins = q.e.dma_start(out=o, in_=i, **kw)
            ds.cnt += 16
            ins.then_inc(ds.sem, 16)
        ev = (ds.name, ds.sem, ds.cnt, None)
        for b in reads:
            b.r[ds.name] = ev
        for b in writes:
            b.w = ev
            b.r = {}
        if store:
            self.stores[ds.name] = ev
        return ev

    def barrier(self, dram_bufs=()):
        sp = self.sp
        for ev in self.stores.values():
            sp.wait(ev)
        self.stores = {}
        for y in self.engs:
            if y is not sp and y.cnt > 0:
                sp.wait((y.name, y.sem, y.cnt, y))
        sp.e.sem_inc(sp.sem, 1)
        sp.cnt += 1
        ev = (sp.name, sp.sem, sp.cnt, None)
        for x in self.engs:
            if x is not sp:
                x.wait(ev)
        for b in dram_bufs:
            b.w = None
            b.r = {}


class T:
    def __init__(self, t, name):
        self.t = t
        self.b = Buf(name)

    def __getitem__(self, idx):
        return self.t[idx]


class Phase:
    count = 0

    def __init__(self, k):
        self.k = k
        self.nc = k.nc
        self.st = contextlib.ExitStack()
        Phase.count += 1
        self.tag = f"p{Phase.count}_"

    def sb(self, name, shape, dt):
        return T(self.st.enter_context(self.nc.sbuf_tensor(self.tag + name, shape, dt)), name)

    def ps(self, name, shape, dt):
        return T(self.st.enter_context(self.nc.psum_tensor(self.tag + name, shape, dt)), name)

    def close(self):
        self.st.close()


def ln_stats(k, x, st6, mv, rstd, half, eps):
    nc = k.nc
    k.op(k.dve, lambda: nc.vector.bn_stats(out=st6[:, 0, :], in_=x[:, 0:512]), [x.b], [st6.b])
    yield
    k.op(k.dve, lambda: nc.vector.bn_stats(out=st6[:, 1, :], in_=x[:, 512:1024]), [x.b], [st6.b])
    yield
    k.op(k.dve, lambda: nc.vector.bn_aggr(out=mv[:, :], in_=st6[:, :, :]), [st6.b], [mv.b])
    k.op(k.pool, lambda: nc.gpsimd.tensor_scalar_add(out=rstd[:, :], in0=mv[:, 1:2], scalar1=eps), [mv.b], [rstd.b])
    k.op(k.pool, lambda: nc.gpsimd.tensor_tensor(out=rstd[:, :], in0=rstd[:, :], in1=half[:, :], op=ALU.pow),
         [rstd.b, half.b], [rstd.b])
    yield


def load_mod_vectors(k, ph, modd, l, r, idx_sh, idx_sc, shP, scP):
    nc = k.nc
    q = k.sp
    src_sh = modd[l, r, idx_sh * D:(idx_sh + 1) * D].rearrange("(c p) -> p c", p=128)
    src_sc = modd[l, r, idx_sc * D:(idx_sc + 1) * D].rearrange("(c p) -> p c", p=128)
    k.dma(q, [(shP[:, :], src_sh)], k.dsem(shP.b.name), [], [shP.b], allow_slow_non_contiguous=True)
    k.dma(q, [(scP[:, :], src_sc)], k.dsem(scP.b.name), [], [scP.b], allow_slow_non_contiguous=True)
    k.op(k.dve, lambda: nc.vector.tensor_scalar_add(out=scP[:, :], in0=scP[:, :], scalar1=1.0), [scP.b], [scP.b])


def load_bcast(k, dst, src_row, scale=None):
    nc = k.nc
    k.dma(k.sp, [(dst[:, :], src_row.broadcast_to([128, D]))], k.dsem(dst.b.name), [], [dst.b])
    if scale is not None:
        k.op(k.dve, lambda: nc.vector.tensor_scalar_mul(out=dst[:, :], in0=dst[:, :], scalar1=float(scale)),
             [dst.b], [dst.b])


def run(gen):
    for _ in gen:
        pass


class BG:
    def __init__(self):
        self.q = []

    def add(self, gen):
        self.q.append(gen)

    def step(self, n=1):
        while n > 0 and self.q:
            try:
                next(self.q[0])
                n -= 1
            except StopIteration:
                self.q.pop(0)

    def flush(self):
        while self.q:
            run(self.q.pop(0))


def prologue_a(k, x, st6, mv, rstd, half, xn):
    nc = k.nc
    yield from ln_stats(k, x, st6, mv, rstd, half, EPS)
    k.op(k.dve, lambda: nc.vector.tensor_scalar(out=xn[:, :], in0=x[:, :], scalar1=mv[:, 0:1], scalar2=rstd[:, 0:1],
                                                op0=ALU.subtract, op1=ALU.mult), [x.b, mv.b, rstd.b], [xn.b])
    yield


def prologue_b(k, xn, psT, ident, hT, col0, shP, scP):
    nc = k.nc
    for c in range(NCH):
        k.op(k.pe, lambda c=c: nc.tensor.transpose(out=psT[:, c, :], in_=xn[:, c * 128:(c + 1) * 128], identity=ident[:, :]),
             [xn.b, ident.b], [psT.b])
    for c in range(NCH):
        k.op(k.act, lambda c=c: nc.scalar.activation(out=hT[:, c, col0:col0 + 128], in_=psT[:, c, :], func=AF.Identity,
                                                     scale=scP[:, c:c + 1], bias=shP[:, c:c + 1]),
             [psT.b, scP.b, shP.b], [hT.b])


def epilogue_sub(k, pys, xres, gb, gamb, betb, r, st6, mv, rstd, half, out_ap, store_name, store_pairs=None,
                 inplace=False):
    nc = k.nc
    acc = xres if inplace else r
    for n in range(2):
        k.op(k.dve, lambda n=n: nc.vector.tensor_tensor(out=r[:, n * 512:(n + 1) * 512], in0=pys[n][:, :],
                                                        in1=gb[:, n * 512:(n + 1) * 512], op=ALU.mult),
             [pys[n].b, gb.b], [r.b])
        yield
    k.op(k.pool, lambda: nc.gpsimd.tensor_tensor(out=acc[:, :], in0=r[:, :], in1=xres[:, :], op=ALU.add),
         [r.b, xres.b], [acc.b])
    yield from ln_stats(k, acc, st6, mv, rstd, half, EPS_A)
    k.op(k.dve, lambda: nc.vector.tensor_scalar(out=acc[:, :], in0=acc[:, :], scalar1=mv[:, 0:1], scalar2=rstd[:, 0:1],
                                                op0=ALU.subtract, op1=ALU.mult), [acc.b, mv.b, rstd.b], [acc.b])
    yield
    k.op(k.pool, lambda: nc.gpsimd.tensor_tensor(out=acc[:, :], in0=acc[:, :], in1=gamb[:, :], op=ALU.mult),
         [acc.b, gamb.b], [acc.b])
    k.op(k.pool, lambda: nc.gpsimd.tensor_tensor(out=acc[:, :], in0=acc[:, :], in1=betb[:, :], op=ALU.add),
         [acc.b, betb.b], [acc.b])
    k.dma(k.sp, store_pairs or [(out_ap, acc[:, :])], k.dsem(acc.b.name if inplace else store_name), [acc.b], [], store=True)


def phase_mod(k, io):
    nc = k.nc
    ph = Phase(k)
    cT = ph.sb("cT", [128, NCH, 2], F32)
    cs = ph.sb("cs", [128, NCH, 2], F32)
    wb = [ph.sb(f"adaw{i}", [128, NCH, 512], F32) for i in range(2)]
    ab = ph.sb("adab", [2, 9216], F32)
    mrow = ph.sb("mrow", [2, 9216], F32)
    pm = [ph.ps(f"pmod{i}", [2, 512], F32) for i in range(2)]
    k.dma(k.sp, [(cT[:, :, :], io["cvecT"])], k.dsem("cT"), [], [cT.b])
    k.op(k.act, lambda: nc.scalar.activation(out=cs[:, :, :], in_=cT[:, :, :], func=AF.Silu), [cT.b], [cs.b])
    it = 0
    for l in range(2):
        k.dma(k.sp, [(ab[:, :], io["ada_b"][l:l + 1, :].broadcast_to([2, 9216]))], k.dsem("adab"), [], [ab.b])
        wv = io["ada_w"][l].rearrange("(c p) n -> p c n", p=128)
        for n in range(18):
            w = wb[it % 2]
            p = pm[it % 2]
            k.dma(k.sp, [(w[:, 0:4, :], wv[:, 0:4, n * 512:(n + 1) * 512]),
                         (w[:, 4:8, :], wv[:, 4:8, n * 512:(n + 1) * 512])], k.dsem(w.b.name), [], [w.b])
            for c in range(NCH):
                k.op(k.pe, lambda c=c, w=w, p=p: nc.tensor.matmul(p[:, :], lhsT=cs[:, c, :], rhs=w[:, c, :],
                                                                 start=(c == 0), stop=(c == NCH - 1)),
                     [cs.b, w.b], [p.b])
            k.op(k.dve, lambda n=n, p=p: nc.vector.tensor_tensor(out=mrow[:, n * 512:(n + 1) * 512], in0=p[:, :],
                                                                in1=ab[:, n * 512:(n + 1) * 512], op=ALU.add),
                 [p.b, ab.b], [mrow.b])
            it += 1
        k.dma(k.sp, [(io["modd"][l], mrow[:, :])], k.dsem("mrow"), [mrow.b], [], store=True)
    k.barrier()
    ph.close()


def phase_ffn(k, io, l, W1, W3, W2, msh, msc, mg, ln_i, xin, xout, ctx_io=None):
    nc = k.nc
    ph = Phase(k)
    TT = 256
    w1s = ph.sb("w1s", [128, NCH, FF], BF16)
    w3s = ph.sb("w3s", [128, NCH, FF], BF16)
    w2s = ph.sb("w2s", [128, NFF, D], BF16)
    ident = ph.sb("ident", [128, 128], BF16)
    half = ph.sb("half", [128, 1], F32)
    gbs = [ph.sb("gb0", [128, D], F32)]
    gamb = ph.sb("gamb", [128, D], F32)
    betb = ph.sb("betb", [128, D], F32)
    shPs = [ph.sb("shP0", [128, NCH], F32)]
    scPs = [ph.sb("scP0", [128, NCH], F32)]
    if ctx_io is not None:
        gbs.append(ph.sb("gb1", [128, D], F32))
        shPs.append(ph.sb("shP1", [128, NCH], F32))
        scPs.append(ph.sb("scP1", [128, NCH], F32))
    hT = [ph.sb(f"hT{i}", [128, NCH, TT], BF16) for i in range(2)]
    u = ph.sb("u", [128, NFF, TT], BF16)
    ubufs = [Buf(f"u{j}") for j in range(NFF)]
    xs = [[ph.sb(f"xs{i}{s}", [128, D], F32) for s in range(2)] for i in range(3)]
    xn = [ph.sb(f"xn{s}", [128, D], BF16) for s in range(2)]
    rr = [ph.sb(f"r0{s}", [128, D], F32) for s in range(2)]
    sg = [ph.sb(f"sg{i}", [128, TT], F32) for i in range(2)]
    st6 = [ph.sb(f"st6{i}", [128, 2, 6], F32) for i in range(4)]
    mv = [ph.sb(f"mv{i}", [128, 2], F32) for i in range(4)]
    rstd = [ph.sb(f"rstd{i}", [128, 1], F32) for i in range(4)]
    psT = [ph.ps(f"psT{i}", [128, NCH, 128], BF16) for i in range(2)]
    ps13 = [ph.ps(f"ps13{i}", [128, 512], F32) for i in range(2)]
    psy = [ph.ps(f"psy{i}", [128, 512], F32) for i in range(4)]

    k.dma(k.sp, [(ident[:, :], io["ident"])], k.dsem("ident"), [], [ident.b])
    k.op(k.dve, lambda: nc.vector.memset(half[:, :], -0.5), [], [half.b])
    modd = io["modd"]
    for r_ in range(len(gbs)):
        load_mod_vectors(k, ph, modd, l, r_, msh, msc, shPs[r_], scPs[r_])
        load_bcast(k, gbs[r_], modd[l, r_:r_ + 1, mg * D:(mg + 1) * D], 0.5 / ALPHA)
    load_bcast(k, gamb, io["ln_g"][l, ln_i:ln_i + 1, :])
    load_bcast(k, betb, io["ln_b"][l, ln_i:ln_i + 1, :])
    W1v = W1.rearrange("(c p) n -> p c n", p=128)
    W3v = W3.rearrange("(c p) n -> p c n", p=128)
    W2v = W2.rearrange("(j p) n -> p j n", p=128)
    for c in range(NCH):
        k.dma(k.pool, [(w1s[:, c, :], W1v[:, c, :])], k.dsem("w1s"), [], [w1s.b])
    for c in range(NCH):
        k.dma(k.pool, [(w3s[:, c, :], W3v[:, c, :])], k.dsem("w3s"), [], [w3s.b])
    for j in range(0, NFF, 2):
        k.dma(k.pool, [(w2s[:, j:j + 2, :], W2v[:, j:j + 2, :])], k.dsem("w2s"), [], [w2s.b])

    tiles = []
    if ctx_io is not None:
        tiles.append((ctx_io[0], ctx_io[1], 0, 1))
    for i in range(L // TT):
        tiles.append((xin, xout, i * TT, 0))
    NT = len(tiles)

    def load(i):
        src, _, row0, _ = tiles[i]
        for s in range(2):
            x = xs[i % 3][s]
            k.dma(k.sp, [(x[:, :], src[row0 + s * 128:row0 + (s + 1) * 128, :])], k.dsem(x.b.name), [], [x.b])

    bg = BG()

    def prologue_A(i):
        for s in range(2):
            yield from prologue_a(k, xs[i % 3][s], st6[s], mv[s], rstd[s], half, xn[s])

    def prologue_B(i):
        ms = tiles[i][3]
        for s in range(2):
            prologue_b(k, xn[s], psT[s], ident, hT[i % 2], s * 128, shPs[ms], scPs[ms])

    def mm13(i):
        h = hT[i % 2]
        for j in range(NFF):
            pb = ps13[j % 2]
            for c in range(NCH):
                k.op(k.pe, lambda c=c, j=j, pb=pb: nc.tensor.matmul(pb[:, 0:TT], lhsT=w1s[:, c, j * 128:(j + 1) * 128],
                                                                   rhs=h[:, c, :], start=(c == 0), stop=(c == NCH - 1)),
                     [w1s.b, h.b], [pb.b])
            for c in range(NCH):
                k.op(k.pe, lambda c=c, j=j, pb=pb: nc.tensor.matmul(pb[:, TT:2 * TT], lhsT=w3s[:, c, j * 128:(j + 1) * 128],
                                                                   rhs=h[:, c, :], start=(c == 0), stop=(c == NCH - 1)),
                     [w3s.b, h.b], [pb.b])
            s_ = sg[j % 2]
            k.op(k.act, lambda pb=pb, s_=s_: nc.scalar.activation(out=s_[:, :], in_=pb[:, 0:TT], func=AF.Silu),
                 [pb.b], [s_.b])
            k.op(k.dve, lambda j=j, pb=pb, s_=s_: nc.vector.tensor_tensor(out=u[:, j, :], in0=s_[:, :], in1=pb[:, TT:2 * TT],
                                                                         op=ALU.mult),
                 [pb.b, s_.b], [ubufs[j]])
            bg.step(1)

    def mm2(i):
        for s in range(2):
            for n in range(2):
                py = psy[2 * s + n]
                for j in range(NFF):
                    k.op(k.pe, lambda s=s, n=n, j=j, py=py: nc.tensor.matmul(
                        py[:, :], lhsT=u[:, j, s * 128:(s + 1) * 128], rhs=w2s[:, j, n * 512:(n + 1) * 512],
                        start=(j == 0), stop=(j == NFF - 1)), [ubufs[j], w2s.b], [py.b])

    def epilogue(i):
        _, dst, row0, ms = tiles[i]
        for s in range(2):
            yield from epilogue_sub(k, [psy[2 * s], psy[2 * s + 1]], xs[i % 3][s], gbs[ms], gamb, betb, rr[s],
                                    st6[2 + s], mv[2 + s], rstd[2 + s], half,
                                    dst[row0 + s * 128:row0 + (s + 1) * 128, :], None, inplace=True)

    load(0)
    if NT > 1:
        load(1)
    run(prologue_A(0))
    prologue_B(0)
    pending = None
    for i in range(NT):
        if i + 1 < NT:
            bg.add(prologue_A(i + 1))
        if pending is not None:
            bg.add(pending)
        mm13(i)
        bg.flush()
        if i + 2 < NT:
            load(i + 2)
        if i + 1 < NT:
            prologue_B(i + 1)
        mm2(i)
        pending = epilogue(i)
    run(pending)
    k.barrier()
    ph.close()


def phase_even_in(k, io, l, xin):
    nc = k.nc
    ph = Phase(k)
    win = ph.sb("win", [128, NCH, 1920], BF16)
    ident = ph.sb("ident", [128, 128], BF16)
    half = ph.sb("half", [128, 1], F32)
    shPs = [ph.sb(f"shP{r}", [128, NCH], F32) for r in range(2)]
    scPs = [ph.sb(f"scP{r}", [128, NCH], F32) for r in range(2)]
    xs = [ph.sb(f"xs{i}0", [128, D], F32) for i in range(2)]
    xn = [ph.sb(f"xn{i}", [128, D], BF16) for i in range(2)]
    hT = [ph.sb(f"hT{i}", [128, NCH, 128], BF16) for i in range(2)]
    rope = [ph.sb(f"rope{i}", [128, 2, 128], F32) for i in range(2)]
    t1 = [ph.sb(f"t1{i}", [128, 128], F32) for i in range(2)]
    t2 = [ph.sb(f"t2{i}", [128, 128], F32) for i in range(2)]
    qkst = [ph.sb(f"qkst{i}", [128, 5, 128], BF16) for i in range(2)]
    pst = [ph.sb(f"pst{i}", [128, 512], BF16) for i in range(2)]
    vst = [ph.sb(f"vst{i}", [128, 128], BF16) for i in range(2)]
    st6 = [ph.sb(f"st6{i}", [128, 2, 6], F32) for i in range(2)]
    mv = [ph.sb(f"mv{i}", [128, 2], F32) for i in range(2)]
    rstd = [ph.sb(f"rstd{i}", [128, 1], F32) for i in range(2)]
    psT = [ph.ps(f"psT{i}", [128, NCH, 128], BF16) for i in range(2)]
    psP = ph.ps("psP", [128, 512], F32)
    psV = ph.ps("psV", [128, 512], F32)
    psQ = [ph.ps(f"psQ{i}", [128, 512], F32) for i in range(2)]

    k.dma(k.sp, [(ident[:, :], io["ident"])], k.dsem("ident"), [], [ident.b])
    k.op(k.dve, lambda: nc.vector.memset(half[:, :], -0.5), [], [half.b])
    for r_ in range(2):
        load_mod_vectors(k, ph, io["modd"], l, r_, 3, 4, shPs[r_], scPs[r_])
    Wv = io["w_in_all"].rearrange("(c p) n -> p c n", p=128)
    for c in range(NCH):
        k.dma(k.pool, [(win[:, c, :], Wv[:, c, :])], k.dsem("win"), [], [win.b])

    blocks = [(io["ctx1"], 0, 1, 0), (io["ctx1"], 128, 1, 1)] + [(xin, n * 128, 0, n) for n in range(NB)]
    NBK = len(blocks)

    def load(i):
        src, row0, ms, n = blocks[i]
        x = xs[i % 2]
        k.dma(k.sp, [(x[:, :], src[row0:row0 + 128, :])], k.dsem(x.b.name), [], [x.b])
        if ms == 0:
            rp = rope[i % 2]
            k.dma(k.sp, [(rp[:, :, :], io["ropeT"][n])], k.dsem(rp.b.name), [], [rp.b])

    def pa(i):
        run(prologue_a(k, xs[i % 2], st6[i % 2], mv[i % 2], rstd[i % 2], half, xn[i % 2]))

    def pb(i):
        ms = blocks[i][2]
        prologue_b(k, xn[i % 2], psT[i % 2], ident, hT[i % 2], 0, shPs[ms], scPs[ms])

    def body(i):
        src, row0, ms, n = blocks[i]
        h = hT[i % 2]
        qs = qkst[i % 2]
        for c in range(NCH):
            k.op(k.pe, lambda c=c: nc.tensor.matmul(psV[:, 0:128], lhsT=h[:, c, :], rhs=win[:, c, 1792:1920],
                                                    start=(c == 0), stop=(c == NCH - 1)), [h.b, win.b], [psV.b])
        vs = vst[i % 2]
        k.op(k.act, lambda: nc.scalar.copy(out=vs[:, :], in_=psV[:, 0:128]), [psV.b], [vs.b])
        if ms == 1:
            k.dma(k.sp, [(io["vcd"][n], vs[:, :])], k.dsem(vs.b.name), [vs.b], [], store=True)
        else:
            k.dma(k.sp, [(io["vd"][n], vs[:, :])], k.dsem(vs.b.name), [vs.b], [], store=True)
        if ms == 0:
            for c in range(NCH):
                k.op(k.pe, lambda c=c: nc.tensor.matmul(psP[:, :], lhsT=h[:, c, :], rhs=win[:, c, 0:512],
                                                        start=(c == 0), stop=(c == NCH - 1)), [h.b, win.b], [psP.b])
            pt = pst[i % 2]
            k.op(k.act, lambda: nc.scalar.copy(out=pt[:, :], in_=psP[:, :]), [psP.b], [pt.b])
            k.dma(k.sp, [(io["pd"][n], pt[:, :])], k.dsem(pt.b.name), [pt.b], [], store=True)
        prs = range(5) if ms == 0 else [4]
        for j, pr in enumerate(prs):
            pq = psQ[j % 2]
            for c in range(NCH):
                k.op(k.pe, lambda c=c, pr=pr, pq=pq: nc.tensor.matmul(
                    pq[:, 0:128], lhsT=win[:, c, 512 + pr * 128:512 + (pr + 1) * 128], rhs=h[:, c, :],
                    start=(c == 0), stop=(c == NCH - 1)), [h.b, win.b], [pq.b])
            if ms == 0:
                for c in range(NCH):
                    k.op(k.pe, lambda c=c, pr=pr, pq=pq: nc.tensor.matmul(
                        pq[:, 128:256], lhsT=win[:, c, 1152 + pr * 128:1152 + (pr + 1) * 128], rhs=h[:, c, :],
                        start=(c == 0), stop=(c == NCH - 1)), [h.b, win.b], [pq.b])
                rp = rope[i % 2]
                a1 = t1[j % 2]
                a2 = t2[j % 2]
                k.op(k.dve, lambda pq=pq, a1=a1: nc.vector.tensor_tensor(out=a1[:, :], in0=pq[:, 0:128], in1=rp[:, 0, :],
                                                                         op=ALU.mult), [pq.b, rp.b], [a1.b])
                k.op(k.dve, lambda pq=pq, a2=a2: nc.vector.tensor_tensor(out=a2[:, :], in0=pq[:, 128:256], in1=rp[:, 1, :],
                                                                         op=ALU.mult), [pq.b, rp.b], [a2.b])
                k.op(k.pool, lambda pr=pr, a1=a1, a2=a2: nc.gpsimd.tensor_tensor(out=qs[:, pr, :], in0=a1[:, :], in1=a2[:, :],
                                                                                op=ALU.add), [a1.b, a2.b], [qs.b])
            else:
                k.op(k.act, lambda pq=pq: nc.scalar.copy(out=qs[:, 4, :], in_=pq[:, 0:128]), [pq.b], [qs.b])
        if ms == 0:
            k.dma(k.sp, [(io["qkd"][n], qs[:, :, :])], k.dsem(qs.b.name), [qs.b], [], store=True)
        else:
            k.dma(k.sp, [(io["kcd"][:, n * 128:(n + 1) * 128], qs[:, 4, :])], k.dsem(qs.b.name), [qs.b], [], store=True)

    load(0)
    pa(0)
    pb(0)
    for i in range(NBK):
        if i + 1 < NBK:
            load(i + 1)
            pa(i + 1)
        body(i)
        if i + 1 < NBK:
            pb(i + 1)
    k.barrier()
    ph.close()


def phase_even_out(k, io, l, xin, xout):
    nc = k.nc
    AXX = mybir.AxisListType.X
    ph = Phase(k)
    ident = ph.sb("ident", [128, 128], BF16)
    half = ph.sb("half", [128, 1], F32)
    kT = ph.sb("kT", [128, (NB + 2) * 128], BF16)
    vall = ph.sb("vall", [128, NB + 2, 128], BF16)
    pall = ph.sb("pall", [128, NB + 2, 512], BF16)
    kcT = ph.sb("kcT", [128, CTX], BF16)
    vc = ph.sb("vc", [128, 2, 128], BF16)
    masks = ph.sb("masks", [128, 3, 384], BF16)
    band = ph.sb("band", [128, 4, 5, 128], BF16)
    rcnt = ph.sb("rcnt", [128, 2, 4, 128], F32)
    wout = ph.sb("wout", [128, NCH, D], BF16)
    poolw = ph.sb("poolw", [128, 4, 128], BF16)
    pscale = ph.sb("pscale", [128, 4], F32)
    sinkc = ph.sb("sinkc", [128, 8], F32)
    nsink = ph.sb("nsink", [128, 8], F32)
    gb = ph.sb("gb0", [128, D], F32)
    gamb = ph.sb("gamb", [128, D], F32)
    betb = ph.sb("betb", [128, D], F32)
    qb = [ph.sb(f"qb{i}", [128, 4, 128], BF16) for i in range(2)]
    xs = [ph.sb(f"xs{i}0", [128, D], F32) for i in range(2)]
    rr = [ph.sb(f"r{i}0", [128, D], F32) for i in range(2)]
    dT = ph.sb("dT", [128, 4, 128], BF16)
    mixT = ph.sb("mixT", [128, NCH, 128], BF16)
    P = [ph.sb(f"P{i}", [128, 640], BF16) for i in range(2)]
    PT = [ph.sb(f"PT{i}", [128, 5, 128], BF16) for i in range(2)]
    mx = [ph.sb(f"mx{h}", [128, 1], F32) for h in range(8)]
    negm = [ph.sb(f"negm{h}", [128, 1], F32) for h in range(8)]
    rs = [ph.sb(f"rs{h}", [128, 1], F32) for h in range(8)]
    es = [ph.sb(f"es{h}", [128, 1], F32) for h in range(8)]
    den = ph.sb("den", [128, 8], F32)
    rden = ph.sb("rden", [128, 8], F32)
    osb = ph.sb("osb", [128, 512], BF16)
    st6 = ph.sb("st60", [128, 2, 6], F32)
    mv = ph.sb("mv0", [128, 2], F32)
    rstd = ph.sb("rstd0", [128, 1], F32)
    S = [ph.ps(f"S{i}", [128, 1024], F32) for i in range(2)]
    PTp = ph.ps("PTp", [128, 5, 128], BF16)
    Ops = ph.ps("Ops", [128, 512], F32)
    psyx = ph.ps("psyx", [128, 1024], F32)
    psy0 = T(psyx.t, "psyx"); psy0.b = psyx.b

    class _V:
        def __init__(self, t, lo, hi, b):
            self.t, self.lo, self.hi, self.b = t, lo, hi, b

        def __getitem__(self, idx):
            return self.t[:, self.lo:self.hi]
    psyh = [_V(psyx.t, 0, 512, psyx.b), _V(psyx.t, 512, 1024, psyx.b)]

    sp = k.sp
    k.dma(sp, [(ident[:, :], io["ident"])], k.dsem("ident"), [], [ident.b])
    k.op(k.dve, lambda: nc.vector.memset(half[:, :], -0.5), [], [half.b])
    k.dma(sp, [(masks[:, :, :], io["masks"])], k.dsem("masks"), [], [masks.b])
    k.dma(sp, [(band[:, :, :, :], io["band"])], k.dsem("band"), [], [band.b])
    k.dma(sp, [(rcnt[:, :, :, :], io["rcnt"].broadcast_to([128, 2, 4, 128]))], k.dsem("rcnt"), [], [rcnt.b])
    k.dma(sp, [(sinkc[:, :], io["attn_sinks"].broadcast_to([128, 8]))], k.dsem("sinkc"), [], [sinkc.b])
    k.op(k.dve, lambda: nc.vector.tensor_scalar_mul(out=nsink[:, :], in0=sinkc[:, :], scalar1=-1.0), [sinkc.b], [nsink.b])
    k.dma(sp, [(pscale[:, :], io["pool_scale"].rearrange("(g p) -> p g", p=128))], k.dsem("pscale"), [], [pscale.b],
          allow_slow_non_contiguous=True)
    k.dma(k.pool, [(poolw[:, :, :], io["pool_w"].rearrange("g c d -> c g d"))], k.dsem("poolw"), [], [poolw.b])
    Wv = io["mix_w_out"].rearrange("(c p) n -> p c n", p=128)
    for c in range(0, NCH, 2):
        k.dma(k.pool, [(wout[:, c:c + 2, :], Wv[:, c:c + 2, :])], k.dsem("wout"), [], [wout.b])
    load_bcast(k, gb, io["modd"][l, 0:1, 5 * D:6 * D], 1.0 / ALPHA)
    load_bcast(k, gamb, io["ln_g"][l, 1:2, :])
    load_bcast(k, betb, io["ln_b"][l, 1:2, :])
    k.op(k.dve, lambda: nc.vector.memset(kT[:, 0:128], 0.0), [], [kT.b])
    k.op(k.dve, lambda: nc.vector.memset(kT[:, (NB + 1) * 128:(NB + 2) * 128], 0.0), [], [kT.b])
    k.op(k.dve, lambda: nc.vector.memset(vall[:, 0, :], 0.0), [], [vall.b])
    k.op(k.dve, lambda: nc.vector.memset(vall[:, NB + 1, :], 0.0), [], [vall.b])
    k.op(k.dve, lambda: nc.vector.memset(pall[:, 0, :], 0.0), [], [pall.b])
    k.op(k.dve, lambda: nc.vector.memset(pall[:, NB + 1, :], 0.0), [], [pall.b])
    kTv = kT.t[:, 128:(NB + 1) * 128].rearrange("p (n t) -> p n t", t=128)
    prs = []
    for n0 in range(0, NB, 16):
        prs.append((kTv[:, n0:n0 + 16, :], io["qkd"][n0:n0 + 16, :, 4, :].rearrange("n p t -> p n t")))
    k.dma(sp, prs, k.dsem("kT"), [], [kT.b])
    prs = []
    for n0 in range(0, NB, 16):
        prs.append((vall[:, 1 + n0:1 + n0 + 16, :], io["vd"][n0:n0 + 16].rearrange("n p d -> p n d")))
    k.dma(sp, prs, k.dsem("vall"), [], [vall.b])
    prs = []
    for n0 in range(0, NB, 8):
        prs.append((pall[:, 1 + n0:1 + n0 + 8, :], io["pd"][n0:n0 + 8].rearrange("n p d -> p n d")))
    k.dma(sp, prs, k.dsem("pall"), [], [pall.b])
    k.dma(sp, [(kcT[:, :], io["kcd"])], k.dsem("kcT"), [], [kcT.b])
    k.dma(sp, [(vc[:, :, :], io["vcd"].rearrange("n p d -> p n d"))], k.dsem("vc"), [], [vc.b])

    bg = BG()

    def loadq(n):
        q = qb[n % 2]
        k.dma(sp, [(q[:, :, :], io["qkd"][n, :, 0:4, :])], k.dsem(q.b.name), [], [q.b])

    def loadx(n):
        x = xs[n % 2]
        k.dma(sp, [(x[:, :], xin[n * 128:(n + 1) * 128, :])], k.dsem(x.b.name), [], [x.b])

    def scores(n, h):
        i, hh = h % 4, h // 4
        b0 = hh * 64
        q = qb[n % 2]
        Sb = S[h % 2]
        mt = 0 if n == 0 else (2 if n == NB - 1 else 1)
        k.op(k.pe, lambda: nc.tensor.matmul(Sb[:, 256:512], lhsT=q[b0:b0 + 64, i, :], rhs=kcT[b0:b0 + 64, :],
                                            start=True, stop=True), [q.b, kcT.b], [Sb.b])
        k.op(k.pe, lambda: nc.tensor.matmul(Sb[:, 512:896], lhsT=q[b0:b0 + 64, i, :], rhs=kT[b0:b0 + 64, n * 128:n * 128 + 384],
                                            start=True, stop=False), [q.b, kT.b], [Sb.b])
        k.op(k.pe, lambda: nc.tensor.matmul(Sb[:, 512:896], lhsT=ident[:, :], rhs=masks[:, mt, :],
                                            start=False, stop=True), [ident.b, masks.b], [Sb.b])

    def front(n, h):
        Sb = S[h % 2]
        Pb = P[h % 2]
        k.op(k.dve, lambda: nc.vector.reduce_max(out=mx[h][:, :], in_=Sb[:, 256:896], axis=AXX), [Sb.b], [mx[h].b])
        k.op(k.dve, lambda: nc.vector.tensor_scalar(out=negm[h][:, :], in0=mx[h][:, :], scalar1=-0.125, scalar2=nsink[:, h:h + 1],
                                                    op0=ALU.mult, op1=ALU.min), [mx[h].b, nsink.b], [negm[h].b])
        k.op(k.act, lambda: nc.scalar.activation(out=Pb[:, :], in_=Sb[:, 256:896], func=AF.Exp, bias=negm[h][:, 0:1],
                                                 scale=0.125, accum_out=rs[h][:, 0:1]), [Sb.b, negm[h].b], [Pb.b, rs[h].b])
        k.op(k.act, lambda: nc.scalar.activation(out=es[h][:, :], in_=negm[h][:, :], func=AF.Exp, bias=sinkc[:, h:h + 1],
                                                 scale=1.0), [negm[h].b, sinkc.b], [es[h].b])

    def back(n, h):
        kvh = h // 4
        Pb = P[h % 2]
        PTb = PT[h % 2]
        for kb in range(5):
            k.op(k.pe, lambda kb=kb: nc.tensor.transpose(out=PTp[:, kb, :], in_=Pb[:, kb * 128:(kb + 1) * 128],
                                                         identity=ident[:, :]), [Pb.b, ident.b], [PTp.b])
        if h % 2 == 0:
            k.op(k.dve, lambda: nc.vector.tensor_copy(out=PTb[:, :, :], in_=PTp[:, :, :]), [PTp.b], [PTb.b])
        else:
            k.op(k.act, lambda: nc.scalar.copy(out=PTb[:, :, :], in_=PTp[:, :, :]), [PTp.b], [PTb.b])
        for kb in range(5):
            if kb < 2:
                rhs = vc[:, kb, kvh * 64:(kvh + 1) * 64]
                rb = vc.b
            else:
                rhs = vall[:, n + kb - 2, kvh * 64:(kvh + 1) * 64]
                rb = vall.b
            k.op(k.pe, lambda kb=kb, rhs=rhs: nc.tensor.matmul(Ops[:, h * 64:(h + 1) * 64], lhsT=PTb[:, kb, :], rhs=rhs,
                                                               start=(kb == 0), stop=(kb == 4)), [PTb.b, rb], [Ops.b])
        k.op(k.dve, lambda: nc.vector.tensor_tensor(out=den[:, h:h + 1], in0=rs[h][:, :], in1=es[h][:, :], op=ALU.add),
             [rs[h].b, es[h].b], [den.b])

    def pooling(n):
        for rnd in range(2):
            for gg in range(2):
                g = rnd * 2 + gg
                for rel in range(3):
                    var = rel
                    if rel == 1 and n == 0:
                        var = 3
                    if rel == 1 and n == NB - 1:
                        var = 4
                    k.op(k.pe, lambda g=g, gg=gg, rel=rel, var=var: nc.tensor.matmul(
                        psyx[:, gg * 128:(gg + 1) * 128], lhsT=pall[:, n + rel, g * 128:(g + 1) * 128], rhs=band[:, g, var, :],
                        start=(rel == 0), stop=(rel == 2)), [pall.b, band.b], [psyx.b])
            for gg in range(2):
                g = rnd * 2 + gg
                w = (2, 4, 8, 16)[g]
                if n == 0 or n == NB - 1:
                    e = 0 if n == 0 else 1
                    k.op(k.dve, lambda g=g, gg=gg, e=e: nc.vector.tensor_tensor(out=dT[:, g, :], in0=psyx[:, gg * 128:(gg + 1) * 128],
                                                                               in1=rcnt[:, e, g, :], op=ALU.mult),
                         [psyx.b, rcnt.b], [dT.b])
                else:
                    k.op(k.dve, lambda g=g, gg=gg, w=w: nc.vector.tensor_scalar_mul(out=dT[:, g, :], in0=psyx[:, gg * 128:(gg + 1) * 128],
                                                                                   scalar1=1.0 / w), [psyx.b], [dT.b])
            for gg in range(2):
                g = rnd * 2 + gg
                k.op(k.pe, lambda g=g, gg=gg: nc.tensor.matmul(psyx[:, 256 + gg * 128:256 + (gg + 1) * 128], lhsT=poolw[:, g, :],
                                                               rhs=dT[:, g, :], start=True, stop=True), [poolw.b, dT.b], [psyx.b])
            for gg in range(2):
                g = rnd * 2 + gg
                k.op(k.act, lambda g=g, gg=gg: nc.scalar.activation(out=mixT[:, g, :], in_=psyx[:, 256 + gg * 128:256 + (gg + 1) * 128],
                                                                   func=AF.Copy, scale=pscale[:, g:g + 1]), [psyx.b, pscale.b], [mixT.b])

    def finish(n):
        k.op(k.dve, lambda: nc.vector.reciprocal(out=rden[:, :], in_=den[:, :]), [den.b], [rden.b])
        k.op(k.dve, lambda: nc.vector.tensor_tensor(
            out=osb[:, :].rearrange("p (h d) -> p h d", d=64), in0=Ops[:, :].rearrange("p (h d) -> p h d", d=64),
            in1=rden[:, :].unsqueeze(2).broadcast_to([128, 8, 64]), op=ALU.mult), [Ops.b, rden.b], [osb.b])
        Sv = psyx
        tp = Sv.t[:, 512:768].bitcast(BF16).rearrange("p (c t) -> p c t", t=128)
        for c in range(4):
            k.op(k.pe, lambda c=c: nc.tensor.transpose(out=tp[:, c, :], in_=osb[:, c * 128:(c + 1) * 128], identity=ident[:, :]),
                 [osb.b, ident.b], [Sv.b])
        k.op(k.act, lambda: nc.scalar.copy(out=mixT[:, 4:8, :], in_=tp[:, :, :]), [Sv.b], [mixT.b])
        for nh in range(2):
            for c in range(NCH):
                k.op(k.pe, lambda c=c, nh=nh: nc.tensor.matmul(psyx[:, nh * 512:(nh + 1) * 512], lhsT=mixT[:, c, :],
                                                               rhs=wout[:, c, nh * 512:(nh + 1) * 512],
                                                               start=(c == 0), stop=(c == NCH - 1)), [mixT.b, wout.b], [psyx.b])

    def epi(n):
        yield from epilogue_sub(k, psyh, xs[n % 2], gb, gamb, betb, rr[n % 2], st6, mv, rstd, half,
                                xout[n * 128:(n + 1) * 128, :], rr[n % 2].b.name)

    loadq(0)
    loadx(0)
    loadx(1)
    scores(0, 0)
    front(0, 0)
    scores(0, 1)
    for n in range(NB):
        if n + 1 < NB:
            loadq(n + 1)
        for h in range(8):
            if h + 1 < 8:
                front(n, h + 1)
            if h + 2 < 8:
                scores(n, h + 2)
            back(n, h)
            bg.step(2)
        bg.flush()
        if n >= 1 and n + 1 < NB:
            loadx(n + 1)
        if n + 1 < NB:
            scores(n + 1, 0)
            front(n + 1, 0)
            scores(n + 1, 1)
        pooling(n)
        finish(n)
        bg.add(epi(n))
    bg.flush()
    k.barrier()
    ph.close()


def phase_odd_in(k, io, l, xin):
    nc = k.nc
    ph = Phase(k)
    ident = ph.sb("ident", [128, 128], BF16)
    half = ph.sb("half", [128, 1], F32)
    shP = ph.sb("shP0", [128, NCH], F32)
    scP = ph.sb("scP0", [128, NCH], F32)
    ttab = ph.sb("ttab", [128, 64, 3, 128], BF16)
    cs256 = ph.sb("cs256", [128, 2, 512], BF16)
    xs = [ph.sb(f"xs{i}0", [128, D], F32) for i in range(2)]
    xn = [ph.sb(f"xn{i}", [128, D], BF16) for i in range(2)]
    hT = [ph.sb(f"hT{i}", [128, NCH, 128], BF16) for i in range(2)]
    ABs = [ph.sb(f"ABs{i}", [128, 2, 4, 256], BF16) for i in range(2)]
    Yst = [ph.sb(f"Yst{i}", [128, 2, D], BF16) for i in range(2)]
    st6 = [ph.sb(f"st6{i}", [128, 2, 6], F32) for i in range(2)]
    mv = [ph.sb(f"mv{i}", [128, 2], F32) for i in range(2)]
    rstd = [ph.sb(f"rstd{i}", [128, 1], F32) for i in range(2)]
    psT = [ph.ps(f"psT{i}", [128, NCH, 128], BF16) for i in range(2)]
    psAB = [ph.ps(f"psAB{i}", [128, 512], F32) for i in range(2)]
    psY = [ph.ps(f"psY{i}", [128, 512], F32) for i in range(2)]
    sp = k.sp
    k.dma(sp, [(ident[:, :], io["ident"])], k.dsem("ident"), [], [ident.b])
    k.op(k.dve, lambda: nc.vector.memset(half[:, :], -0.5), [], [half.b])
    load_mod_vectors(k, ph, io["modd"], l, 0, 3, 4, shP, scP)
    k.dma(sp, [(ttab[:, 0:32, :, :], io["ttab"][:, 0:32, :, :]), (ttab[:, 32:64, :, :], io["ttab"][:, 32:64, :, :])],
          k.dsem("ttab"), [], [ttab.b])
    k.dma(sp, [(cs256[:, :, :], io["cs256"].rearrange("(kc p) n -> p kc n", p=128))], k.dsem("cs256"), [], [cs256.b])
    xv = xin.rearrange("(p j) d -> j p d", j=64)
    ydv = io["yd"]

    def load(j):
        x = xs[j % 2]
        k.dma(sp, [(x[:, :], xv[j])], k.dsem(x.b.name), [], [x.b])

    def pa(j):
        run(prologue_a(k, xs[j % 2], st6[j % 2], mv[j % 2], rstd[j % 2], half, xn[j % 2]))

    def pb(j):
        prologue_b(k, xn[j % 2], psT[j % 2], ident, hT[j % 2], 0, shP, scP)

    def body(j):
        h = hT[j % 2]
        ab = ABs[j % 2]
        for g in range(4):
            pab = psAB[g % 2]
            for kc in range(2):
                k.op(k.pe, lambda g=g, kc=kc, pab=pab: nc.tensor.matmul(pab[:, :], lhsT=h[:, 2 * g + kc, :], rhs=cs256[:, kc, :],
                                                                       start=(kc == 0), stop=(kc == 1)), [h.b, cs256.b], [pab.b])
            eng = k.act if g % 2 == 0 else k.dve
            if g % 2 == 0:
                k.op(k.act, lambda g=g, pab=pab: nc.scalar.copy(out=ab[:, :, g, :], in_=pab[:, :].rearrange("p (r c) -> p r c", r=2)),
                     [pab.b], [ab.b])
            else:
                k.op(k.dve, lambda g=g, pab=pab: nc.vector.tensor_copy(out=ab[:, :, g, :], in_=pab[:, :].rearrange("p (r c) -> p r c", r=2)),
                     [pab.b], [ab.b])
    def body2(j):
        ab = ABs[j % 2]
        ys = Yst[j % 2]
        A = ab.t[:, 0, :, :].rearrange("p g c -> p (g c)")
        B = ab.t[:, 1, :, :].rearrange("p g c -> p (g c)")
        it = 0
        for ri in range(2):
            for ch in range(2):
                py = psY[it % 2]
                it += 1
                cs_ = slice(ch * 512, (ch + 1) * 512)
                ta, tb = (0, 2) if ri == 0 else (1, 0)
                k.op(k.pe, lambda py=py, ta=ta, cs_=cs_: nc.tensor.matmul(py[:, :], lhsT=ttab[:, j, ta, :], rhs=A[:, cs_],
                                                                         start=True, stop=False), [ttab.b, ab.b], [py.b])
                k.op(k.pe, lambda py=py, tb=tb, cs_=cs_: nc.tensor.matmul(py[:, :], lhsT=ttab[:, j, tb, :], rhs=B[:, cs_],
                                                                         start=False, stop=True), [ttab.b, ab.b], [py.b])
                if it % 2 == 0:
                    k.op(k.act, lambda py=py, ri=ri, cs_=cs_: nc.scalar.copy(out=ys[:, ri, cs_], in_=py[:, :]), [py.b], [ys.b])
                else:
                    k.op(k.dve, lambda py=py, ri=ri, cs_=cs_: nc.vector.tensor_copy(out=ys[:, ri, cs_], in_=py[:, :]), [py.b], [ys.b])
        k.dma(sp, [(ydv[0, j], ys[:, 0, :]), (ydv[1, j], ys[:, 1, :])], k.dsem(ys.b.name), [ys.b], [], store=True)

    load(0)
    pa(0)
    pb(0)
    for j in range(64):
        if j + 1 < 64:
            load(j + 1)
            pa(j + 1)
        body(j)
        if j + 1 < 64:
            pb(j + 1)
        body2(j)
    k.barrier()
    ph.close()


def phase_odd_out(k, io, l, xin, xout):
    nc = k.nc
    ph = Phase(k)
    half = ph.sb("half", [128, 1], F32)
    w2stk = ph.sb("w2stk", [128, 64], BF16)
    fw = ph.sb("wout", [128, NCH, D], BF16)
    gb = ph.sb("gb0", [128, D], F32)
    gamb = ph.sb("gamb", [128, D], F32)
    betb = ph.sb("betb", [128, D], F32)
    Ystk = [ph.sb(f"Ystk{i}", [128, 2, D], BF16) for i in range(2)]
    fT = [ph.sb(f"fT{i}", [128, NCH, 128], BF16) for i in range(2)]
    xs = [ph.sb(f"xs{i}0", [128, D], F32) for i in range(2)]
    rr = [ph.sb(f"r{i}0", [128, D], F32) for i in range(2)]
    st6 = ph.sb("st60", [128, 2, 6], F32)
    mv = ph.sb("mv0", [128, 2], F32)
    rstd = ph.sb("rstd0", [128, 1], F32)
    psF = [ph.ps(f"psF{i}", [128, NCH, 128], F32) for i in range(2)]
    psy = [[ph.ps(f"psy{i}{nh}", [128, 512], F32) for nh in range(2)] for i in range(2)]
    sp = k.sp
    k.op(k.dve, lambda: nc.vector.memset(half[:, :], -0.5), [], [half.b])
    k.dma(sp, [(w2stk[:, :], io["w2stk"])], k.dsem("w2stk"), [], [w2stk.b])
    Wv = io["fourier_w_out"].rearrange("(c p) n -> p c n", p=128)
    for c in range(0, NCH, 2):
        k.dma(k.pool, [(fw[:, c:c + 2, :], Wv[:, c:c + 2, :])], k.dsem("wout"), [], [fw.b])
    load_bcast(k, gb, io["modd"][l, 0:1, 5 * D:6 * D], 1.0 / ALPHA)
    load_bcast(k, gamb, io["ln_g"][l, 1:2, :])
    load_bcast(k, betb, io["ln_b"][l, 1:2, :])
    xv = xin.rearrange("(k2 k1) d -> k1 k2 d", k1=128)
    ov = xout.rearrange("(k2 k1) d -> k1 k2 d", k1=128)
    ydv = io["yd"]
    fscale = float(1.0 / np.sqrt(L * 256.0))
    bg = BG()

    def loady(m):
        y = Ystk[m % 2]
        k.dma(sp, [(y[0:64, :, :], ydv[0, :, 2 * m:2 * m + 2, :]), (y[64:128, :, :], ydv[1, :, 2 * m:2 * m + 2, :])],
              k.dsem(y.b.name), [], [y.b])

    def loadx(m):
        x = xs[m % 2]
        k.dma(sp, [(x[0:64, :], xv[2 * m]), (x[64:128, :], xv[2 * m + 1])], k.dsem(x.b.name), [], [x.b])

    def stage2(m):
        y = Ystk[m % 2]
        pf = psF[m % 2]
        f = fT[m % 2]
        for par in range(2):
            for c in range(NCH):
                k.op(k.pe, lambda par=par, c=c: nc.tensor.matmul(pf[:, c, par * 64:(par + 1) * 64],
                                                                 lhsT=y[:, par, c * 128:(c + 1) * 128], rhs=w2stk[:, :],
                                                                 start=True, stop=True), [y.b, w2stk.b], [pf.b])
        k.op(k.act, lambda: nc.scalar.mul(out=f[:, 0:4, :], in_=pf[:, 0:4, :], mul=fscale), [pf.b], [f.b])
        k.op(k.dve, lambda: nc.vector.tensor_scalar_mul(out=f[:, 4:8, :], in0=pf[:, 4:8, :], scalar1=fscale), [pf.b], [f.b])

    def wout(m):
        f = fT[m % 2]
        for nh in range(2):
            py = psy[m % 2][nh]
            for c in range(NCH):
                k.op(k.pe, lambda c=c, nh=nh, py=py: nc.tensor.matmul(py[:, :], lhsT=f[:, c, :], rhs=fw[:, c, nh * 512:(nh + 1) * 512],
                                                                     start=(c == 0), stop=(c == NCH - 1)), [f.b, fw.b], [py.b])

    def epi(m):
        r = rr[m % 2]
        yield from epilogue_sub(k, psy[m % 2], xs[m % 2], gb, gamb, betb, r, st6, mv, rstd, half, None, r.b.name, store_pairs=[
            (ov[2 * m], r[0:64, :]), (ov[2 * m + 1], r[64:128, :])])

    loady(0)
    loady(1)
    loadx(0)
    loadx(1)
    stage2(0)
    for m in range(64):
        wout(m)
        if m + 1 < 64:
            stage2(m + 1)
        if m + 2 < 64:
            loady(m + 2)
        run(epi(m))
        if m + 2 < 64:
            loadx(m + 2)
    k.barrier()
    ph.close()


def build(nphases=99, dbg=()):
    nc = bass.Bass("TRN2", target_bir_lowering=False)
    io = {}

    def inp(name, shape, dt=F32):
        io[name] = nc.dram_tensor(name, list(shape), dt, kind="ExternalInput").ap()

    def scr(name, shape, dt=F32):
        io[name] = nc.dram_tensor(name, list(shape), dt, kind="ExternalOutput" if name in dbg else "Internal").ap()

    inp("x", [L, D]); inp("ctx", [CTX, D]); inp("cvecT", [128, NCH, 2])
    inp("ada_w", [2, D, 9 * D]); inp("ada_b", [2, 9 * D])
    inp("ln_g", [2, 3, D]); inp("ln_b", [2, 3, D])
    for nm in ("ffn1_w1", "ffn1_w3", "ffn2_w1", "ffn2_w3"):
        inp(nm, [2, D, FF])
    for nm in ("ffn1_w2", "ffn2_w2"):
        inp(nm, [2, FF, D])
    inp("w_in_all", [D, 1920]); inp("pool_w", [4, 128, 128]); inp("pool_scale", [512]); inp("attn_sinks", [1, 8])
    inp("mix_w_out", [D, D]); inp("fourier_w_out", [D, D])
    inp("ident", [128, 128], BF16)
    inp("ropeT", [NB, 128, 2, 128]); inp("masks", [128, 3, 384], BF16); inp("band", [128, 4, 5, 128], BF16)
    inp("rcnt", [1, 2, 4, 128]); inp("ttab", [128, 64, 3, 128], BF16); inp("cs256", [256, 512], BF16)
    inp("w2stk", [128, 64], BF16)
    io["out"] = nc.dram_tensor("out", [L, D], F32, kind="ExternalOutput").ap()
    scr("modd", [2, 2, 9 * D])
    scr("xa", [L, D]); scr("xb", [L, D]); scr("xc", [L, D]); scr("ctx1", [CTX, D])
    scr("pd", [NB, 128, 512], BF16); scr("qkd", [NB, 128, 5, 128], BF16); scr("vd", [NB, 128, 128], BF16)
    scr("kcd", [128, CTX], BF16); scr("vcd", [2, 128, 128], BF16)
    scr("yd", [2, 64, 128, D], BF16)

    k = Kern(nc)
    F = lambda nm, l: io[nm][l]
    phases = [
        lambda dst: phase_mod(k, io),
        lambda dst: phase_ffn(k, io, 0, F("ffn1_w1", 0), F("ffn1_w3", 0), F("ffn1_w2", 0), 0, 1, 2, 0,
                              io["x"], dst or io["xa"], (io["ctx"], io["ctx1"])),
        lambda dst: phase_even_in(k, io, 0, io["xa"]),
        lambda dst: phase_even_out(k, io, 0, io["xa"], dst or io["xb"]),
        lambda dst: phase_ffn(k, io, 0, F("ffn2_w1", 0), F("ffn2_w3", 0), F("ffn2_w2", 0), 6, 7, 8, 2,
                              io["xb"], dst or io["xc"]),
        lambda dst: phase_ffn(k, io, 1, F("ffn1_w1", 1), F("ffn1_w3", 1), F("ffn1_w2", 1), 0, 1, 2, 0,
                              io["xc"], dst or io["xa"]),
        lambda dst: phase_odd_in(k, io, 1, io["xa"]),
        lambda dst: phase_odd_out(k, io, 1, io["xa"], dst or io["xb"]),
        lambda dst: phase_ffn(k, io, 1, F("ffn2_w1", 1), F("ffn2_w3", 1), F("ffn2_w2", 1), 6, 7, 8, 2,
                              io["xb"], dst or io["out"]),
    ]
    n = min(nphases, len(phases))
    for i in range(n):
        with nc.named_scope(f"phase{i}"):
            phases[i](io["out"] if i == n - 1 and i > 0 else None)
    return nc


def make_consts():
    c = {}
    c["ident"] = np.eye(128, dtype=np.float32).astype(NPBF)
    t = np.arange(L)
    row = (t // 64).astype(np.float64)
    col = (t % 64).astype(np.float64)
    freqs = (10000.0 ** (-np.arange(0, 32, 2, dtype=np.float32) / 32.0)).astype(np.float32).astype(np.float64)
    d = np.arange(64)
    dd = d % 32
    fi = dd % 16
    pos = np.where(d[:, None] < 32, row[None, :], col[None, :])
    ang = (pos.astype(np.float32) * freqs[fi][:, None].astype(np.float32)).astype(np.float64)
    sign = np.where(dd < 16, -1.0, 1.0)[:, None]
    cosT = np.cos(ang)
    sinT = np.sin(ang) * sign
    tab = np.stack([cosT, sinT], axis=1)
    tab = np.concatenate([tab, tab], axis=0)
    c["ropeT"] = np.ascontiguousarray(tab.reshape(128, 2, NB, 128).transpose(2, 0, 1, 3)).astype(np.float32)
    qi = np.arange(128)[:, None]
    ki = np.arange(384)[None, :]
    base = np.abs(ki - 128 - qi) <= 128
    m = np.zeros((128, 3, 384), np.float32)
    m[:, 0] = np.where(base & (ki >= 128), 0.0, NEG)
    m[:, 1] = np.where(base, 0.0, NEG)
    m[:, 2] = np.where(base & (ki < 256), 0.0, NEG)
    c["masks"] = m.astype(NPBF)
    band = np.zeros((128, 4, 5, 128), np.float32)
    rc = np.zeros((1, 2, 4, 128), np.float32)
    src = np.arange(128)[:, None]
    dst = np.arange(128)[None, :]
    for g, w in enumerate((2, 4, 8, 16)):
        hw = w // 2
        inwin = lambda s_glob: ((s_glob >= dst - hw) & (s_glob <= dst + hw - 1)).astype(np.float32)
        eye = (src == dst).astype(np.float32)
        band[:, g, 0] = inwin(src - 128)
        band[:, g, 1] = inwin(src) - w * eye
        band[:, g, 2] = inwin(src + 128)
        cnt_first = np.minimum(dst + hw, L) - np.maximum(dst - hw, 0)
        tg = (L - 128) + dst
        cnt_last = np.minimum(tg + hw, L) - np.maximum(tg - hw, 0)
        band[:, g, 3] = inwin(src) - cnt_first * eye
        band[:, g, 4] = inwin(src) - cnt_last * eye
        rc[0, 0, g] = 1.0 / cnt_first[0]
        rc[0, 1, g] = 1.0 / cnt_last[0]
    c["band"] = band.astype(NPBF)
    c["rcnt"] = rc
    n1 = np.arange(128)[:, None, None]
    j = np.arange(64)[None, :, None]
    k1 = np.arange(128)[None, None, :]
    th = 2.0 * np.pi * (((64 * n1 + j) * k1) % L) / L
    c["ttab"] = np.stack([np.cos(th), np.sin(th), -np.sin(th)], axis=2).astype(np.float32).astype(NPBF)
    cc = np.arange(256)
    th2 = 2.0 * np.pi * ((cc[:, None] * cc[None, :]) % 256) / 256.0
    c["cs256"] = np.concatenate([np.cos(th2), np.sin(th2)], axis=1).astype(np.float32).astype(NPBF)
    jj = np.arange(64)
    th3 = 2.0 * np.pi * ((jj[:, None] * jj[None, :]) % 64) / 64.0
    c["w2stk"] = np.concatenate([np.cos(th3), -np.sin(th3)], axis=0).astype(np.float32).astype(NPBF)
    return c


def layout_w_in(w_in):
    w_in = np.asarray(w_in, dtype=np.float32)
    p = w_in[:, 0:512]
    q = w_in[:, 512:1024].reshape(D, 8, 64)
    kk = w_in[:, 1024:1152].reshape(D, 2, 64)
    v = w_in[:, 1152:1280]
    pairs = [np.concatenate([q[:, i], q[:, i + 4]], axis=1) for i in range(4)] + [np.concatenate([kk[:, 0], kk[:, 1]], axis=1)]
    qk = np.concatenate(pairs, axis=1)
    d = np.arange(64)
    partner = np.where((d % 32) < 16, d + 16, d - 16)
    idx = np.concatenate([blk * 64 + partner for blk in range(10)])
    qksw = qk[:, idx]
    return np.ascontiguousarray(np.concatenate([p, qk, qksw, v], axis=1))


def make_in_maps(inputs, cores=range(8)):
    consts = make_consts()
    f32 = lambda a: np.ascontiguousarray(np.asarray(a, dtype=np.float32))
    shared = {}
    for nm in ("ada_w", "ada_b", "ln_g", "ln_b", "ffn1_w1", "ffn1_w3", "ffn1_w2", "ffn2_w1", "ffn2_w3", "ffn2_w2"):
        shared[nm] = f32(inputs[nm])
    shared["w_in_all"] = layout_w_in(inputs["mix_w_in"][0])
    shared["pool_w"] = f32(inputs["pool_w"][0])
    shared["pool_scale"] = f32(inputs["pool_scale"][0])
    shared["attn_sinks"] = f32(inputs["attn_sinks"][0]).reshape(1, 8)
    shared["mix_w_out"] = f32(inputs["mix_w_out"][0])
    shared["fourier_w_out"] = f32(inputs["fourier_w_out"][0])
    shared.update(consts)
    x = inputs["x"]; ctx = inputs["ctx"]; c = f32(inputs["c"]); cc = f32(inputs["c_ctx"])
    in_maps = []
    for b in cores:
        m = dict(shared)
        m["x"] = f32(x[b])
        m["ctx"] = f32(ctx[b])
        cv = np.stack([c[b], cc], axis=0)
        m["cvecT"] = np.ascontiguousarray(cv.reshape(2, NCH, 128).transpose(2, 1, 0))
        in_maps.append(m)
    return in_maps


def kernel(**inputs):
    nc = build()
    in_maps = make_in_maps(inputs)
    res = run_bass_kernel_spmd(nc, in_maps, core_ids=list(range(8)))
    return np.stack([r["out"] for r in res.results], axis=0)
```

```python
import contextlib
import numpy as np
import ml_dtypes
import concourse.bass as bass
import concourse.mybir as mybir
from concourse.bass_utils import run_bass_kernel_spmd

F32 = mybir.dt.float32
BF16 = mybir.dt.bfloat16
AF = mybir.ActivationFunctionType
ALU = mybir.AluOpType
NPBF = ml_dtypes.bfloat16

D = 1024
L = 8192
FF = 2816
NFF = 22
NCH = 8
CTX = 256
NB = 64
ALPHA = 4.0 ** 0.25
EPS = 1e-5
EPS_A = EPS / (ALPHA * ALPHA)
NEG = -30000.0


class Eng:
    def __init__(self, nc, name, eng, is_pe=False):
        self.name = name
        self.e = eng
        self.is_pe = is_pe
        self.sem = nc.alloc_semaphore(name="es_" + name)
        self.cnt = 0
        self.waited = {}

    def wait(self, ev):
        key, sem, val, _ = ev
        if self.waited.get(key, 0) >= val:
            return
        self.e.wait_ge(sem, val)
        self.waited[key] = val


class Buf:
    def __init__(self, name):
        self.name = name
        self.w = None
        self.r = {}


class DSem:
    def __init__(self, nc, name):
        self.name = name
        self.sem = nc.alloc_semaphore(name="ds_" + name)
        self.cnt = 0


class Kern:
    def __init__(self, nc):
        self.nc = nc
        self.pe = Eng(nc, "pe", nc.tensor, is_pe=True)
        self.act = Eng(nc, "act", nc.scalar)
        self.dve = Eng(nc, "dve", nc.vector)
        self.pool = Eng(nc, "pool", nc.gpsimd)
        self.sp = Eng(nc, "sp", nc.sync)
        self.engs = [self.pe, self.act, self.dve, self.pool, self.sp]
        self.stores = {}
        self.dsems = {}

    def dsem(self, name):
        if name not in self.dsems:
            self.dsems[name] = DSem(self.nc, name)
        return self.dsems[name]

    def _deps(self, eng, reads, writes):
        for b in reads:
            if b.w is not None:
                self._need(eng, b.w, True)
        for b in writes:
            if b.w is not None:
                self._need(eng, b.w, False)
            for ev in b.r.values():
                self._need(eng, ev, False)

    def _need(self, eng, ev, raw):
        src = ev[3]
        if src is eng:
            if eng.is_pe or not raw:
                return
        eng.wait(ev)

    def op(self, eng, fn, reads=(), writes=()):
        self._deps(eng, reads, writes)
        ins = fn()
        eng.cnt += 1
        ins.then_inc(eng.sem, 1)
        ev = (eng.name, eng.sem, eng.cnt, eng)
        for b in reads:
            b.r[eng.name] = ev
        for b in writes:
            b.w = ev
            b.r = {}
        return ev

    def dma(self, q, pairs, ds, reads=(), writes=(), store=False, **kw):
        self._deps(q, reads, writes)
        for (o, i) in pairs:
            ins = q.e.dma_start(out=o, in_=i, **kw)
            ds.cnt += 16
            ins.then_inc(ds.sem, 16)
        ev = (ds.name, ds.sem, ds.cnt, None)
        for b in reads:
            b.r[ds.name] = ev
        for b in writes:
            b.w = ev
            b.r = {}
        if store:
            self.stores[ds.name] = ev
        return ev

    def barrier(self, dram_bufs=()):
        sp = self.sp
        for ev in self.stores.values():
            sp.wait(ev)
        self.stores = {}
        for y in self.engs:
            if y is not sp and y.cnt > 0:
                sp.wait((y.name, y.sem, y.cnt, y))
        sp.e.sem_inc(sp.sem, 1)
        sp.cnt += 1
        ev = (sp.name, sp.sem, sp.cnt, None)
        for x in self.engs:
            if x is not sp:
                x.wait(ev)
        for b in dram_bufs:
            b.w = None
            b.r = {}


class T:
    def __init__(self, t, name):
        self.t = t
        self.b = Buf(name)

    def __getitem__(self, idx):
        return self.t[idx]


class Phase:
    count = 0

    def __init__(self, k):
        self.k = k
        self.nc = k.nc
        self.st = contextlib.ExitStack()
        Phase.count += 1
        self.tag = f"p{Phase.count}_"

    def sb(self, name, shape, dt):
        return T(self.st.enter_context(self.nc.sbuf_tensor(self.tag + name, shape, dt)), name)

    def ps(self, name, shape, dt):
        return T(self.st.enter_context(self.nc.psum_tensor(self.tag + name, shape, dt)), name)

    def close(self):
        self.st.close()


def ln_stats(k, x, st6, mv, rstd, half, eps):
    nc = k.nc
    k.op(k.dve, lambda: nc.vector.bn_stats(out=st6[:, 0, :], in_=x[:, 0:512]), [x.b], [st6.b])
    yield
    k.op(k.dve, lambda: nc.vector.bn_stats(out=st6[:, 1, :], in_=x[:, 512:1024]), [x.b], [st6.b])
    yield
    k.op(k.dve, lambda: nc.vector.bn_aggr(out=mv[:, :], in_=st6[:, :, :]), [st6.b], [mv.b])
    k.op(k.pool, lambda: nc.gpsimd.tensor_scalar_add(out=rstd[:, :], in0=mv[:, 1:2], scalar1=eps), [mv.b], [rstd.b])
    k.op(k.pool, lambda: nc.gpsimd.tensor_tensor(out=rstd[:, :], in0=rstd[:, :], in1=half[:, :], op=ALU.pow),
         [rstd.b, half.b], [rstd.b])
    yield


def load_mod_vectors(k, ph, modd, l, r, idx_sh, idx_sc, shP, scP):
    nc = k.nc
    q = k.sp
    src_sh = modd[l, r, idx_sh * D:(idx_sh + 1) * D].rearrange("(c p) -> p c", p=128)
    src_sc = modd[l, r, idx_sc * D:(idx_sc + 1) * D].rearrange("(c p) -> p c", p=128)
    k.dma(q, [(shP[:, :], src_sh)], k.dsem(shP.b.name), [], [shP.b], allow_slow_non_contiguous=True)
    k.dma(q, [(scP[:, :], src_sc)], k.dsem(scP.b.name), [], [scP.b], allow_slow_non_contiguous=True)
    k.op(k.dve, lambda: nc.vector.tensor_scalar_add(out=scP[:, :], in0=scP[:, :], scalar1=1.0), [scP.b], [scP.b])


def load_bcast(k, dst, src_row, scale=None):
    nc = k.nc
    k.dma(k.sp, [(dst[:, :], src_row.broadcast_to([128, D]))], k.dsem(dst.b.name), [], [dst.b])
    if scale is not None:
        k.op(k.dve, lambda: nc.vector.tensor_scalar_mul(out=dst[:, :], in0=dst[:, :], scalar1=float(scale)),
             [dst.b], [dst.b])


def run(gen):
    for _ in gen:
        pass


class BG:
    def __init__(self):
        self.q = []

    def add(self, gen):
        self.q.append(gen)

    def step(self, n=1):
        while n > 0 and self.q:
            try:
                next(self.q[0])
                n -= 1
            except StopIteration:
                self.q.pop(0)

    def flush(self):
        while self.q:
            run(self.q.pop(0))


def prologue_a(k, x, st6, mv, rstd, half, xn):
    nc = k.nc
    yield from ln_stats(k, x, st6, mv, rstd, half, EPS)
    k.op(k.dve, lambda: nc.vector.tensor_scalar(out=xn[:, :], in0=x[:, :], scalar1=mv[:, 0:1], scalar2=rstd[:, 0:1],
                                                op0=ALU.subtract, op1=ALU.mult), [x.b, mv.b, rstd.b], [xn.b])
    yield


def prologue_b(k, xn, psT, ident, hT, col0, shP, scP):
    nc = k.nc
    for c in range(NCH):
        k.op(k.pe, lambda c=c: nc.tensor.transpose(out=psT[:, c, :], in_=xn[:, c * 128:(c + 1) * 128], identity=ident[:, :]),
             [xn.b, ident.b], [psT.b])
    for c in range(NCH):
        k.op(k.act, lambda c=c: nc.scalar.activation(out=hT[:, c, col0:col0 + 128], in_=psT[:, c, :], func=AF.Identity,
                                                     scale=scP[:, c:c + 1], bias=shP[:, c:c + 1]),
             [psT.b, scP.b, shP.b], [hT.b])


def epilogue_sub(k, pys, xres, gb, gamb, betb, r, st6, mv, rstd, half, out_ap, store_name, store_pairs=None,
                 inplace=False):
    nc = k.nc
    acc = xres if inplace else r
    for n in range(2):
        k.op(k.dve, lambda n=n: nc.vector.tensor_tensor(out=r[:, n * 512:(n + 1) * 512], in0=pys[n][:, :],
                                                        in1=gb[:, n * 512:(n + 1) * 512], op=ALU.mult),
             [pys[n].b, gb.b], [r.b])
        yield
    k.op(k.pool, lambda: nc.gpsimd.tensor_tensor(out=acc[:, :], in0=r[:, :], in1=xres[:, :], op=ALU.add),
         [r.b, xres.b], [acc.b])
    yield
    yield from ln_stats(k, acc, st6, mv, rstd, half, EPS_A)
    k.op(k.dve, lambda: nc.vector.tensor_scalar(out=acc[:, :], in0=acc[:, :], scalar1=mv[:, 0:1], scalar2=rstd[:, 0:1],
                                                op0=ALU.subtract, op1=ALU.mult), [acc.b, mv.b, rstd.b], [acc.b])
    yield
    k.op(k.pool, lambda: nc.gpsimd.tensor_tensor(out=acc[:, :], in0=acc[:, :], in1=gamb[:, :], op=ALU.mult),
         [acc.b, gamb.b], [acc.b])
    k.op(k.pool, lambda: nc.gpsimd.tensor_tensor(out=acc[:, :], in0=acc[:, :], in1=betb[:, :], op=ALU.add),
         [acc.b, betb.b], [acc.b])
    k.dma(k.sp, store_pairs or [(out_ap, acc[:, :])], k.dsem(acc.b.name if inplace else store_name), [acc.b], [], store=True)


def phase_mod(k, io):
    nc = k.nc
    ph = Phase(k)
    cT = ph.sb("cT", [128, NCH, 2], F32)
    cs = ph.sb("cs", [128, NCH, 2], F32)
    wb = [ph.sb(f"adaw{i}", [128, NCH, 512], F32) for i in range(2)]
    ab = ph.sb("adab", [2, 9216], F32)
    mrow = ph.sb("mrow", [2, 9216], F32)
    pm = [ph.ps(f"pmod{i}", [2, 512], F32) for i in range(2)]
    k.dma(k.sp, [(cT[:, :, :], io["cvecT"])], k.dsem("cT"), [], [cT.b])
    k.op(k.act, lambda: nc.scalar.activation(out=cs[:, :, :], in_=cT[:, :, :], func=AF.Silu), [cT.b], [cs.b])
    it = 0
    for l in range(2):
        k.dma(k.sp, [(ab[:, :], io["ada_b"][l:l + 1, :].broadcast_to([2, 9216]))], k.dsem("adab"), [], [ab.b])
        wv = io["ada_w"][l].rearrange("(c p) n -> p c n", p=128)
        for n in range(18):
            w = wb[it % 2]
            p = pm[it % 2]
            k.dma(k.sp, [(w[:, 0:4, :], wv[:, 0:4, n * 512:(n + 1) * 512]),
                         (w[:, 4:8, :], wv[:, 4:8, n * 512:(n + 1) * 512])], k.dsem(w.b.name), [], [w.b])
            for c in range(NCH):
                k.op(k.pe, lambda c=c, w=w, p=p: nc.tensor.matmul(p[:, :], lhsT=cs[:, c, :], rhs=w[:, c, :],
                                                                 start=(c == 0), stop=(c == NCH - 1)),
                     [cs.b, w.b], [p.b])
            k.op(k.dve, lambda n=n, p=p: nc.vector.tensor_tensor(out=mrow[:, n * 512:(n + 1) * 512], in0=p[:, :],
                                                                in1=ab[:, n * 512:(n + 1) * 512], op=ALU.add),
                 [p.b, ab.b], [mrow.b])
            it += 1
        k.dma(k.sp, [(io["modd"][l], mrow[:, :])], k.dsem("mrow"), [mrow.b], [], store=True)
    k.barrier()
    ph.close()


def phase_ffn(k, io, l, W1, W3, W2, msh, msc, mg, ln_i, xin, xout, ctx_io=None):
    nc = k.nc
    ph = Phase(k)
    TT = 256
    w1s = ph.sb("w1s", [128, NCH, FF], BF16)
    w3s = ph.sb("w3s", [128, NCH, FF], BF16)
    w2s = ph.sb("w2s", [128, NFF, D], BF16)
    ident = ph.sb("ident", [128, 128], BF16)
    half = ph.sb("half", [128, 1], F32)
    gbs = [ph.sb("gb0", [128, D], F32)]
    gamb = ph.sb("gamb", [128, D], F32)
    betb = ph.sb("betb", [128, D], F32)
    shPs = [ph.sb("shP0", [128, NCH], F32)]
    scPs = [ph.sb("scP0", [128, NCH], F32)]
    if ctx_io is not None:
        gbs.append(ph.sb("gb1", [128, D], F32))
        shPs.append(ph.sb("shP1", [128, NCH], F32))
        scPs.append(ph.sb("scP1", [128, NCH], F32))
    hT = [ph.sb(f"hT{i}", [128, NCH, TT], BF16) for i in range(2)]
    u = ph.sb("u", [128, NFF, TT], BF16)
    ubufs = [Buf(f"u{j}") for j in range(NFF)]
    xs = [[ph.sb(f"xs{i}{s}", [128, D], F32) for s in range(2)] for i in range(3)]
    xn = [ph.sb(f"xn{s}", [128, D], BF16) for s in range(2)]
    rr = [ph.sb(f"r0{s}", [128, D], F32) for s in range(2)]
    sg = [ph.sb(f"sg{i}", [128, TT], F32) for i in range(2)]
    st6 = [ph.sb(f"st6{i}", [128, 2, 6], F32) for i in range(4)]
    mv = [ph.sb(f"mv{i}", [128, 2], F32) for i in range(4)]
    rstd = [ph.sb(f"rstd{i}", [128, 1], F32) for i in range(4)]
    psT = [ph.ps(f"psT{i}", [128, NCH, 128], BF16) for i in range(2)]
    ps13 = [ph.ps(f"ps13{i}", [128, 512], F32) for i in range(2)]
    psy = [ph.ps(f"psy{i}", [128, 512], F32) for i in range(4)]

    k.dma(k.sp, [(ident[:, :], io["ident"])], k.dsem("ident"), [], [ident.b])
    k.op(k.dve, lambda: nc.vector.memset(half[:, :], -0.5), [], [half.b])
    modd = io["modd"]
    for r_ in range(len(gbs)):
        load_mod_vectors(k, ph, modd, l, r_, msh, msc, shPs[r_], scPs[r_])
        load_bcast(k, gbs[r_], modd[l, r_:r_ + 1, mg * D:(mg + 1) * D], 0.5 / ALPHA)
    load_bcast(k, gamb, io["ln_g"][l, ln_i:ln_i + 1, :])
    load_bcast(k, betb, io["ln_b"][l, ln_i:ln_i + 1, :])
    W1v = W1.rearrange("(c p) n -> p c n", p=128)
    W3v = W3.rearrange("(c p) n -> p c n", p=128)
    W2v = W2.rearrange("(j p) n -> p j n", p=128)
    for c in range(NCH):
        k.dma(k.pool, [(w1s[:, c, :], W1v[:, c, :])], k.dsem("w1s"), [], [w1s.b])
    for c in range(NCH):
        k.dma(k.pool, [(w3s[:, c, :], W3v[:, c, :])], k.dsem("w3s"), [], [w3s.b])
    for j in range(0, NFF, 2):
        k.dma(k.pool, [(w2s[:, j:j + 2, :], W2v[:, j:j + 2, :])], k.dsem("w2s"), [], [w2s.b])

    tiles = []
    if ctx_io is not None:
        tiles.append((ctx_io[0], ctx_io[1], 0, 1))
    for i in range(L // TT):
        tiles.append((xin, xout, i * TT, 0))
    NT = len(tiles)

    def load(i):
        src, _, row0, _ = tiles[i]
        for s in range(2):
            x = xs[i % 3][s]
            k.dma(k.sp, [(x[:, :], src[row0 + s * 128:row0 + (s + 1) * 128, :])], k.dsem(x.b.name), [], [x.b])

    bg = BG()

    def prologue_A(i):
        for s in range(2):
            yield from prologue_a(k, xs[i % 3][s], st6[s], mv[s], rstd[s], half, xn[s])

    def prologue_B(i):
        ms = tiles[i][3]
        for s in range(2):
            prologue_b(k, xn[s], psT[s], ident, hT[i % 2], s * 128, shPs[ms], scPs[ms])

    def mm13(i):
        h = hT[i % 2]
        for j in range(NFF):
            pb = ps13[j % 2]
            for c in range(NCH):
                k.op(k.pe, lambda c=c, j=j, pb=pb: nc.tensor.matmul(pb[:, 0:TT], lhsT=w1s[:, c, j * 128:(j + 1) * 128],
                                                                   rhs=h[:, c, :], start=(c == 0), stop=(c == NCH - 1)),
                     [w1s.b, h.b], [pb.b])
            for c in range(NCH):
                k.op(k.pe, lambda c=c, j=j, pb=pb: nc.tensor.matmul(pb[:, TT:2 * TT], lhsT=w3s[:, c, j * 128:(j + 1) * 128],
                                                                   rhs=h[:, c, :], start=(c == 0), stop=(c == NCH - 1)),
                     [w3s.b, h.b], [pb.b])
            s_ = sg[j % 2]
            k.op(k.act, lambda pb=pb, s_=s_: nc.scalar.activation(out=s_[:, :], in_=pb[:, 0:TT], func=AF.Silu),
                 [pb.b], [s_.b])
            k.op(k.dve, lambda j=j, pb=pb, s_=s_: nc.vector.tensor_tensor(out=u[:, j, :], in0=s_[:, :], in1=pb[:, TT:2 * TT],
                                                                         op=ALU.mult),
                 [pb.b, s_.b], [ubufs[j]])
            bg.step(1)

    def mm2(i):
        for s in range(2):
            for n in range(2):
                py = psy[2 * s + n]
                for j in range(NFF):
                    k.op(k.pe, lambda s=s, n=n, j=j, py=py: nc.tensor.matmul(
                        py[:, :], lhsT=u[:, j, s * 128:(s + 1) * 128], rhs=w2s[:, j, n * 512:(n + 1) * 512],
                        start=(j == 0), stop=(j == NFF - 1)), [ubufs[j], w2s.b], [py.b])

    def epilogue(i):
        _, dst, row0, ms = tiles[i]
        for s in range(2):
            yield from epilogue_sub(k, [psy[2 * s], psy[2 * s + 1]], xs[i % 3][s], gbs[ms], gamb, betb, rr[s],
                                    st6[2 + s], mv[2 + s], rstd[2 + s], half,
                                    dst[row0 + s * 128:row0 + (s + 1) * 128, :], None, inplace=True)

    load(0)
    if NT > 1:
        load(1)
    run(prologue_A(0))
    prologue_B(0)
    pending = None
    for i in range(NT):
        if i + 1 < NT:
            bg.add(prologue_A(i + 1))
        if pending is not None:
            bg.add(pending)
        mm13(i)
        bg.flush()
        if i + 2 < NT:
            load(i + 2)
        if i + 1 < NT:
            prologue_B(i + 1)
        mm2(i)
        pending = epilogue(i)
    run(pending)
    k.barrier()
    ph.close()


def phase_even_in(k, io, l, xin):
    nc = k.nc
    ph = Phase(k)
    win = ph.sb("win", [128, NCH, 1920], BF16)
    ident = ph.sb("ident", [128, 128], BF16)
    half = ph.sb("half", [128, 1], F32)
    shPs = [ph.sb(f"shP{r}", [128, NCH], F32) for r in range(2)]
    scPs = [ph.sb(f"scP{r}", [128, NCH], F32) for r in range(2)]
    xs = [ph.sb(f"xs{i}0", [128, D], F32) for i in range(3)]
    xn = [ph.sb(f"xn{i}", [128, D], BF16) for i in range(3)]
    hT = [ph.sb(f"hT{i}", [128, NCH, 128], BF16) for i in range(2)]
    rope = [ph.sb(f"rope{i}", [128, 2, 128], F32) for i in range(2)]
    t1 = [ph.sb(f"t1{i}", [128, 128], F32) for i in range(2)]
    t2 = [ph.sb(f"t2{i}", [128, 128], F32) for i in range(2)]
    qkst = [ph.sb(f"qkst{i}", [128, 5, 128], BF16) for i in range(2)]
    pst = [ph.sb(f"pst{i}", [128, 512], BF16) for i in range(2)]
    vst = [ph.sb(f"vst{i}", [128, 128], BF16) for i in range(2)]
    st6 = [ph.sb(f"st6{i}", [128, 2, 6], F32) for i in range(3)]
    mv = [ph.sb(f"mv{i}", [128, 2], F32) for i in range(3)]
    rstd = [ph.sb(f"rstd{i}", [128, 1], F32) for i in range(3)]
    psT = [ph.ps(f"psT{i}", [128, NCH, 128], BF16) for i in range(2)]
    psP = ph.ps("psP", [128, 512], F32)
    psV = ph.ps("psV", [128, 512], F32)
    psQ = [ph.ps(f"psQ{i}", [128, 512], F32) for i in range(2)]

    k.dma(k.sp, [(ident[:, :], io["ident"])], k.dsem("ident"), [], [ident.b])
    k.op(k.dve, lambda: nc.vector.memset(half[:, :], -0.5), [], [half.b])
    for r_ in range(2):
        load_mod_vectors(k, ph, io["modd"], l, r_, 3, 4, shPs[r_], scPs[r_])
    Wv = io["w_in_all"].rearrange("(c p) n -> p c n", p=128)
    for c in range(NCH):
        k.dma(k.pool, [(win[:, c, :], Wv[:, c, :])], k.dsem("win"), [], [win.b])

    blocks = [(io["ctx1"], 0, 1, 0), (io["ctx1"], 128, 1, 1)] + [(xin, n * 128, 0, n) for n in range(NB)]
    NBK = len(blocks)

    def load(i):
        src, row0, ms, n = blocks[i]
        x = xs[i % 3]
        k.dma(k.sp, [(x[:, :], src[row0:row0 + 128, :])], k.dsem(x.b.name), [], [x.b])

    def loadr(i):
        src, row0, ms, n = blocks[i]
        if ms == 0:
            rp = rope[i % 2]
            k.dma(k.sp, [(rp[:, :, :], io["ropeT"][n])], k.dsem(rp.b.name), [], [rp.b])

    def pa(i):
        run(prologue_a(k, xs[i % 3], st6[i % 3], mv[i % 3], rstd[i % 3], half, xn[i % 3]))

    def pb(i):
        ms = blocks[i][2]
        prologue_b(k, xn[i % 3], psT[i % 2], ident, hT[i % 2], 0, shPs[ms], scPs[ms])

    def body(i):
        src, row0, ms, n = blocks[i]
        h = hT[i % 2]
        qs = qkst[i % 2]
        for c in range(NCH):
            k.op(k.pe, lambda c=c: nc.tensor.matmul(psV[:, 0:128], lhsT=h[:, c, :], rhs=win[:, c, 1792:1920],
                                                    start=(c == 0), stop=(c == NCH - 1)), [h.b, win.b], [psV.b])
        vs = vst[i % 2]
        k.op(k.act, lambda: nc.scalar.copy(out=vs[:, :], in_=psV[:, 0:128]), [psV.b], [vs.b])
        if ms == 1:
            k.dma(k.sp, [(io["vcd"][n], vs[:, :])], k.dsem(vs.b.name), [vs.b], [], store=True)
        else:
            k.dma(k.sp, [(io["vd"][n], vs[:, :])], k.dsem(vs.b.name), [vs.b], [], store=True)
        if ms == 0:
            for c in range(NCH):
                k.op(k.pe, lambda c=c: nc.tensor.matmul(psP[:, :], lhsT=h[:, c, :], rhs=win[:, c, 0:512],
                                                        start=(c == 0), stop=(c == NCH - 1)), [h.b, win.b], [psP.b])
            pt = pst[i % 2]
            k.op(k.act, lambda: nc.scalar.copy(out=pt[:, :], in_=psP[:, :]), [psP.b], [pt.b])
            k.dma(k.sp, [(io["pd"][n], pt[:, :])], k.dsem(pt.b.name), [pt.b], [], store=True)
    def body2(i):
        src, row0, ms, n = blocks[i]
        h = hT[i % 2]
        qs = qkst[i % 2]
        prs = range(5) if ms == 0 else [4]
        for j, pr in enumerate(prs):
            pq = psQ[j % 2]
            for c in range(NCH):
                k.op(k.pe, lambda c=c, pr=pr, pq=pq: nc.tensor.matmul(
                    pq[:, 0:128], lhsT=win[:, c, 512 + pr * 128:512 + (pr + 1) * 128], rhs=h[:, c, :],
                    start=(c == 0), stop=(c == NCH - 1)), [h.b, win.b], [pq.b])
            if ms == 0:
                for c in range(NCH):
                    k.op(k.pe, lambda c=c, pr=pr, pq=pq: nc.tensor.matmul(
                        pq[:, 128:256], lhsT=win[:, c, 1152 + pr * 128:1152 + (pr + 1) * 128], rhs=h[:, c, :],
                        start=(c == 0), stop=(c == NCH - 1)), [h.b, win.b], [pq.b])
                rp = rope[i % 2]
                a1 = t1[j % 2]
                a2 = t2[j % 2]
                k.op(k.dve, lambda pq=pq, a1=a1: nc.vector.tensor_tensor(out=a1[:, :], in0=pq[:, 0:128], in1=rp[:, 0, :],
                                                                         op=ALU.mult), [pq.b, rp.b], [a1.b])
                k.op(k.dve, lambda pq=pq, a2=a2: nc.vector.tensor_tensor(out=a2[:, :], in0=pq[:, 128:256], in1=rp[:, 1, :],
                                                                         op=ALU.mult), [pq.b, rp.b], [a2.b])
                k.op(k.pool, lambda pr=pr, a1=a1, a2=a2: nc.gpsimd.tensor_tensor(out=qs[:, pr, :], in0=a1[:, :], in1=a2[:, :],
                                                                                op=ALU.add), [a1.b, a2.b], [qs.b])
            else:
                k.op(k.act, lambda pq=pq: nc.scalar.copy(out=qs[:, 4, :], in_=pq[:, 0:128]), [pq.b], [qs.b])
        if ms == 0:
            k.dma(k.sp, [(io["qkd"][n], qs[:, :, :])], k.dsem(qs.b.name), [qs.b], [], store=True)
        else:
            k.dma(k.sp, [(io["kcd"][:, n * 128:(n + 1) * 128], qs[:, 4, :])], k.dsem(qs.b.name), [qs.b], [], store=True)

    load(0)
    load(1)
    load(2)
    loadr(0)
    pa(0)
    pa(1)
    pb(0)
    for i in range(NBK):
        if i + 3 < NBK:
            load(i + 3)
        if i + 1 < NBK:
            loadr(i + 1)
        if i + 2 < NBK:
            pa(i + 2)
        body(i)
        if i + 1 < NBK:
            pb(i + 1)
        body2(i)
    k.barrier()
    ph.close()


def phase_even_out(k, io, l, xin, xout):
    nc = k.nc
    AXX = mybir.AxisListType.X
    ph = Phase(k)
    ident = ph.sb("ident", [128, 128], BF16)
    half = ph.sb("half", [128, 1], F32)
    kT = ph.sb("kT", [128, (NB + 2) * 128], BF16)
    vall = ph.sb("vall", [128, NB + 2, 128], BF16)
    pall = ph.sb("pall", [128, NB + 2, 512], BF16)
    kcT = ph.sb("kcT", [128, CTX], BF16)
    vc = ph.sb("vc", [128, 2, 128], BF16)
    masks = ph.sb("masks", [128, 3, 384], BF16)
    band = ph.sb("band", [128, 4, 5, 128], BF16)
    rcnt = ph.sb("rcnt", [128, 2, 4, 128], F32)
    wout = ph.sb("wout", [128, NCH, D], BF16)
    poolw = ph.sb("poolw", [128, 4, 128], BF16)
    pscale = ph.sb("pscale", [128, 4], F32)
    sinkc = ph.sb("sinkc", [128, 8], F32)
    nsink = ph.sb("nsink", [128, 8], F32)
    gb = ph.sb("gb0", [128, D], F32)
    gamb = ph.sb("gamb", [128, D], F32)
    betb = ph.sb("betb", [128, D], F32)
    qb = [ph.sb(f"qb{i}", [128, 4, 128], BF16) for i in range(2)]
    xs = [ph.sb(f"xs{i}0", [128, D], F32) for i in range(2)]
    rr = [ph.sb(f"r{i}0", [128, D], F32) for i in range(2)]
    dT = ph.sb("dT", [128, 4, 128], BF16)
    mixT = ph.sb("mixT", [128, NCH, 128], BF16)
    P = [ph.sb(f"P{i}", [128, 640], BF16) for i in range(2)]
    PT = [ph.sb(f"PT{i}", [128, 5, 128], BF16) for i in range(2)]
    mx = [ph.sb(f"mx{h}", [128, 1], F32) for h in range(8)]
    negm = [ph.sb(f"negm{h}", [128, 1], F32) for h in range(8)]
    rs = [ph.sb(f"rs{h}", [128, 1], F32) for h in range(8)]
    es = [ph.sb(f"es{h}", [128, 1], F32) for h in range(8)]
    den = ph.sb("den", [128, 8], F32)
    rden = ph.sb("rden", [128, 8], F32)
    osb = ph.sb("osb", [128, 512], BF16)
    st6 = ph.sb("st60", [128, 2, 6], F32)
    mv = ph.sb("mv0", [128, 2], F32)
    rstd = ph.sb("rstd0", [128, 1], F32)
    S = [ph.ps(f"S{i}", [128, 1024], F32) for i in range(2)]
    PTp = ph.ps("PTp", [128, 5, 128], BF16)
    Ops = ph.ps("Ops", [128, 512], F32)
    psyx = ph.ps("psyx", [128, 1024], F32)
    psy0 = T(psyx.t, "psyx"); psy0.b = psyx.b

    class _V:
        def __init__(self, t, lo, hi, b):
            self.t, self.lo, self.hi, self.b = t, lo, hi, b

        def __getitem__(self, idx):
            return self.t[:, self.lo:self.hi]
    psyh = [_V(psyx.t, 0, 512, psyx.b), _V(psyx.t, 512, 1024, psyx.b)]

    sp = k.sp
    k.dma(sp, [(ident[:, :], io["ident"])], k.dsem("ident"), [], [ident.b])
    k.op(k.dve, lambda: nc.vector.memset(half[:, :], -0.5), [], [half.b])
    k.dma(sp, [(masks[:, :, :], io["masks"])], k.dsem("masks"), [], [masks.b])
    k.dma(sp, [(band[:, :, :, :], io["band"])], k.dsem("band"), [], [band.b])
    k.dma(sp, [(rcnt[:, :, :, :], io["rcnt"].broadcast_to([128, 2, 4, 128]))], k.dsem("rcnt"), [], [rcnt.b])
    k.dma(sp, [(sinkc[:, :], io["attn_sinks"].broadcast_to([128, 8]))], k.dsem("sinkc"), [], [sinkc.b])
    k.op(k.dve, lambda: nc.vector.tensor_scalar_mul(out=nsink[:, :], in0=sinkc[:, :], scalar1=-1.0), [sinkc.b], [nsink.b])
    k.dma(sp, [(pscale[:, :], io["pool_scale"].rearrange("(g p) -> p g", p=128))], k.dsem("pscale"), [], [pscale.b],
          allow_slow_non_contiguous=True)
    k.dma(k.pool, [(poolw[:, :, :], io["pool_w"].rearrange("g c d -> c g d"))], k.dsem("poolw"), [], [poolw.b])
    Wv = io["mix_w_out"].rearrange("(c p) n -> p c n", p=128)
    for c in range(0, NCH, 2):
        k.dma(k.pool, [(wout[:, c:c + 2, :], Wv[:, c:c + 2, :])], k.dsem("wout"), [], [wout.b])
    load_bcast(k, gb, io["modd"][l, 0:1, 5 * D:6 * D], 1.0 / ALPHA)
    load_bcast(k, gamb, io["ln_g"][l, 1:2, :])
    load_bcast(k, betb, io["ln_b"][l, 1:2, :])
    k.op(k.dve, lambda: nc.vector.memset(kT[:, 0:128], 0.0), [], [kT.b])
    k.op(k.dve, lambda: nc.vector.memset(kT[:, (NB + 1) * 128:(NB + 2) * 128], 0.0), [], [kT.b])
    k.op(k.dve, lambda: nc.vector.memset(vall[:, 0, :], 0.0), [], [vall.b])
    k.op(k.dve, lambda: nc.vector.memset(vall[:, NB + 1, :], 0.0), [], [vall.b])
    k.op(k.dve, lambda: nc.vector.memset(pall[:, 0, :], 0.0), [], [pall.b])
    k.op(k.dve, lambda: nc.vector.memset(pall[:, NB + 1, :], 0.0), [], [pall.b])
    kTv = kT.t[:, 128:(NB + 1) * 128].rearrange("p (n t) -> p n t", t=128)
    prs = []
    for n0 in range(0, NB, 16):
        prs.append((kTv[:, n0:n0 + 16, :], io["qkd"][n0:n0 + 16, :, 4, :].rearrange("n p t -> p n t")))
    k.dma(sp, prs, k.dsem("kT"), [], [kT.b])
    prs = []
    for n0 in range(0, NB, 16):
        prs.append((vall[:, 1 + n0:1 + n0 + 16, :], io["vd"][n0:n0 + 16].rearrange("n p d -> p n d")))
    k.dma(sp, prs, k.dsem("vall"), [], [vall.b])
    prs = []
    for n0 in range(0, NB, 8):
        prs.append((pall[:, 1 + n0:1 + n0 + 8, :], io["pd"][n0:n0 + 8].rearrange("n p d -> p n d")))
    k.dma(sp, prs, k.dsem("pall"), [], [pall.b])
    k.dma(sp, [(kcT[:, :], io["kcd"])], k.dsem("kcT"), [], [kcT.b])
    k.dma(sp, [(vc[:, :, :], io["vcd"].rearrange("n p d -> p n d"))], k.dsem("vc"), [], [vc.b])

    bg = BG()

    def loadq(n):
        q = qb[n % 2]
        k.dma(sp, [(q[:, :, :], io["qkd"][n, :, 0:4, :])], k.dsem(q.b.name), [], [q.b])

    def loadx(n):
        x = xs[n % 2]
        k.dma(sp, [(x[:, :], xin[n * 128:(n + 1) * 128, :])], k.dsem(x.b.name), [], [x.b])

    def scores(n, h):
        i, hh = h % 4, h // 4
        b0 = hh * 64
        q = qb[n % 2]
        Sb = S[h % 2]
        mt = 0 if n == 0 else (2 if n == NB - 1 else 1)
        k.op(k.pe, lambda: nc.tensor.matmul(Sb[:, 256:512], lhsT=q[b0:b0 + 64, i, :], rhs=kcT[b0:b0 + 64, :],
                                            start=True, stop=True), [q.b, kcT.b], [Sb.b])
        k.op(k.pe, lambda: nc.tensor.matmul(Sb[:, 512:896], lhsT=q[b0:b0 + 64, i, :], rhs=kT[b0:b0 + 64, n * 128:n * 128 + 384],
                                            start=True, stop=False), [q.b, kT.b], [Sb.b])
        k.op(k.pe, lambda: nc.tensor.matmul(Sb[:, 512:896], lhsT=ident[:, :], rhs=masks[:, mt, :],
                                            start=False, stop=True), [ident.b, masks.b], [Sb.b])

    def front(n, h):
        Sb = S[h % 2]
        Pb = P[h % 2]
        k.op(k.dve, lambda: nc.vector.reduce_max(out=mx[h][:, :], in_=Sb[:, 256:896], axis=AXX), [Sb.b], [mx[h].b])
        k.op(k.dve, lambda: nc.vector.tensor_scalar(out=negm[h][:, :], in0=mx[h][:, :], scalar1=-0.125, scalar2=nsink[:, h:h + 1],
                                                    op0=ALU.mult, op1=ALU.min), [mx[h].b, nsink.b], [negm[h].b])
        k.op(k.act, lambda: nc.scalar.activation(out=Pb[:, :], in_=Sb[:, 256:896], func=AF.Exp, bias=negm[h][:, 0:1],
                                                 scale=0.125, accum_out=rs[h][:, 0:1]), [Sb.b, negm[h].b], [Pb.b, rs[h].b])
        k.op(k.act, lambda: nc.scalar.activation(out=es[h][:, :], in_=negm[h][:, :], func=AF.Exp, bias=sinkc[:, h:h + 1],
                                                 scale=1.0), [negm[h].b, sinkc.b], [es[h].b])

    def back_t(n, h):
        Pb = P[h % 2]
        for kb in range(5):
            k.op(k.pe, lambda kb=kb: nc.tensor.transpose(out=PTp[:, kb, :], in_=Pb[:, kb * 128:(kb + 1) * 128],
                                                         identity=ident[:, :]), [Pb.b, ident.b], [PTp.b])

    def back_c(n, h):
        PTb = PT[h % 2]
        if h % 2 == 0:
            k.op(k.dve, lambda: nc.vector.tensor_copy(out=PTb[:, :, :], in_=PTp[:, :, :]), [PTp.b], [PTb.b])
        else:
            k.op(k.act, lambda: nc.scalar.copy(out=PTb[:, :, :], in_=PTp[:, :, :]), [PTp.b], [PTb.b])
        k.op(k.dve, lambda: nc.vector.tensor_tensor(out=den[:, h:h + 1], in0=rs[h][:, :], in1=es[h][:, :], op=ALU.add),
             [rs[h].b, es[h].b], [den.b])

    def back_pv(n, h):
        kvh = h // 4
        PTb = PT[h % 2]
        for kb in range(5):
            if kb < 2:
                rhs = vc[:, kb, kvh * 64:(kvh + 1) * 64]
                rb = vc.b
            else:
                rhs = vall[:, n + kb - 2, kvh * 64:(kvh + 1) * 64]
                rb = vall.b
            k.op(k.pe, lambda kb=kb, rhs=rhs: nc.tensor.matmul(Ops[:, h * 64:(h + 1) * 64], lhsT=PTb[:, kb, :], rhs=rhs,
                                                               start=(kb == 0), stop=(kb == 4)), [PTb.b, rb], [Ops.b])

    def pooling(n):
        for rnd in range(2):
            for gg in range(2):
                g = rnd * 2 + gg
                for rel in range(3):
                    var = rel
                    if rel == 1 and n == 0:
                        var = 3
                    if rel == 1 and n == NB - 1:
                        var = 4
                    k.op(k.pe, lambda g=g, gg=gg, rel=rel, var=var: nc.tensor.matmul(
                        psyx[:, gg * 128:(gg + 1) * 128], lhsT=pall[:, n + rel, g * 128:(g + 1) * 128], rhs=band[:, g, var, :],
                        start=(rel == 0), stop=(rel == 2)), [pall.b, band.b], [psyx.b])
            for gg in range(2):
                g = rnd * 2 + gg
                w = (2, 4, 8, 16)[g]
                if n == 0 or n == NB - 1:
                    e = 0 if n == 0 else 1
                    k.op(k.dve, lambda g=g, gg=gg, e=e: nc.vector.tensor_tensor(out=dT[:, g, :], in0=psyx[:, gg * 128:(gg + 1) * 128],
                                                                               in1=rcnt[:, e, g, :], op=ALU.mult),
                         [psyx.b, rcnt.b], [dT.b])
                else:
                    k.op(k.dve, lambda g=g, gg=gg, w=w: nc.vector.tensor_scalar_mul(out=dT[:, g, :], in0=psyx[:, gg * 128:(gg + 1) * 128],
                                                                                   scalar1=1.0 / w), [psyx.b], [dT.b])
            for gg in range(2):
                g = rnd * 2 + gg
                k.op(k.pe, lambda g=g, gg=gg: nc.tensor.matmul(psyx[:, 256 + gg * 128:256 + (gg + 1) * 128], lhsT=poolw[:, g, :],
                                                               rhs=dT[:, g, :], start=True, stop=True), [poolw.b, dT.b], [psyx.b])
            for gg in range(2):
                g = rnd * 2 + gg
                k.op(k.act, lambda g=g, gg=gg: nc.scalar.activation(out=mixT[:, g, :], in_=psyx[:, 256 + gg * 128:256 + (gg + 1) * 128],
                                                                   func=AF.Copy, scale=pscale[:, g:g + 1]), [psyx.b, pscale.b], [mixT.b])

    def finish(n):
        k.op(k.dve, lambda: nc.vector.reciprocal(out=rden[:, :], in_=den[:, :]), [den.b], [rden.b])
        k.op(k.dve, lambda: nc.vector.tensor_tensor(
            out=osb[:, :].rearrange("p (h d) -> p h d", d=64), in0=Ops[:, :].rearrange("p (h d) -> p h d", d=64),
            in1=rden[:, :].unsqueeze(2).broadcast_to([128, 8, 64]), op=ALU.mult), [Ops.b, rden.b], [osb.b])
        Sv = psyx
        tp = Sv.t[:, 512:768].bitcast(BF16).rearrange("p (c t) -> p c t", t=128)
        for c in range(4):
            k.op(k.pe, lambda c=c: nc.tensor.transpose(out=tp[:, c, :], in_=osb[:, c * 128:(c + 1) * 128], identity=ident[:, :]),
                 [osb.b, ident.b], [Sv.b])
        k.op(k.act, lambda: nc.scalar.copy(out=mixT[:, 4:8, :], in_=tp[:, :, :]), [Sv.b], [mixT.b])
        for nh in range(2):
            for c in range(NCH):
                k.op(k.pe, lambda c=c, nh=nh: nc.tensor.matmul(psyx[:, nh * 512:(nh + 1) * 512], lhsT=mixT[:, c, :],
                                                               rhs=wout[:, c, nh * 512:(nh + 1) * 512],
                                                               start=(c == 0), stop=(c == NCH - 1)), [mixT.b, wout.b], [psyx.b])

    def epi(n):
        yield from epilogue_sub(k, psyh, xs[n % 2], gb, gamb, betb, rr[n % 2], st6, mv, rstd, half,
                                xout[n * 128:(n + 1) * 128, :], rr[n % 2].b.name)

    loadq(0)
    loadx(0)
    loadx(1)
    scores(0, 0)
    front(0, 0)
    scores(0, 1)
    for n in range(NB):
        if n + 1 < NB:
            loadq(n + 1)
        for h in range(8):
            if h + 1 < 8:
                front(n, h + 1)
            back_t(n, h)
            if h + 2 < 8:
                scores(n, h + 2)
            back_c(n, h)
            if h >= 1:
                back_pv(n, h - 1)
            bg.step(2)
        back_pv(n, 7)
        bg.flush()
        if n >= 1 and n + 1 < NB:
            loadx(n + 1)
        if n + 1 < NB:
            scores(n + 1, 0)
            front(n + 1, 0)
            scores(n + 1, 1)
        pooling(n)
        finish(n)
        bg.add(epi(n))
    bg.flush()
    k.barrier()
    ph.close()


def phase_odd_in(k, io, l, xin):
    nc = k.nc
    ph = Phase(k)
    ident = ph.sb("ident", [128, 128], BF16)
    half = ph.sb("half", [128, 1], F32)
    shP = ph.sb("shP0", [128, NCH], F32)
    scP = ph.sb("scP0", [128, NCH], F32)
    ttab = ph.sb("ttab", [128, 64, 3, 128], BF16)
    cs256 = ph.sb("cs256", [128, 2, 512], BF16)
    xs = [ph.sb(f"xs{i}0", [128, D], F32) for i in range(3)]
    xn = [ph.sb(f"xn{i}", [128, D], BF16) for i in range(3)]
    hT = [ph.sb(f"hT{i}", [128, NCH, 128], BF16) for i in range(2)]
    ABs = [ph.sb(f"ABs{i}", [128, 2, 4, 256], BF16) for i in range(2)]
    Yst = [ph.sb(f"Yst{i}", [128, 2, D], BF16) for i in range(2)]
    st6 = [ph.sb(f"st6{i}", [128, 2, 6], F32) for i in range(3)]
    mv = [ph.sb(f"mv{i}", [128, 2], F32) for i in range(3)]
    rstd = [ph.sb(f"rstd{i}", [128, 1], F32) for i in range(3)]
    psT = [ph.ps(f"psT{i}", [128, NCH, 128], BF16) for i in range(2)]
    psAB = [ph.ps(f"psAB{i}", [128, 512], F32) for i in range(2)]
    psY = [ph.ps(f"psY{i}", [128, 512], F32) for i in range(2)]
    sp = k.sp
    k.dma(sp, [(ident[:, :], io["ident"])], k.dsem("ident"), [], [ident.b])
    k.op(k.dve, lambda: nc.vector.memset(half[:, :], -0.5), [], [half.b])
    load_mod_vectors(k, ph, io["modd"], l, 0, 3, 4, shP, scP)
    k.dma(sp, [(ttab[:, 0:32, :, :], io["ttab"][:, 0:32, :, :]), (ttab[:, 32:64, :, :], io["ttab"][:, 32:64, :, :])],
          k.dsem("ttab"), [], [ttab.b])
    k.dma(sp, [(cs256[:, :, :], io["cs256"].rearrange("(kc p) n -> p kc n", p=128))], k.dsem("cs256"), [], [cs256.b])
    xv = xin.rearrange("(p j) d -> j p d", j=64)
    ydv = io["yd"]

    def load(j):
        x = xs[j % 3]
        k.dma(sp, [(x[:, :], xv[j])], k.dsem(x.b.name), [], [x.b])

    def pa(j):
        run(prologue_a(k, xs[j % 3], st6[j % 3], mv[j % 3], rstd[j % 3], half, xn[j % 3]))

    def pb(j):
        prologue_b(k, xn[j % 3], psT[j % 2], ident, hT[j % 2], 0, shP, scP)

    def body(j):
        h = hT[j % 2]
        ab = ABs[j % 2]
        for g in range(4):
            pab = psAB[g % 2]
            for kc in range(2):
                k.op(k.pe, lambda g=g, kc=kc, pab=pab: nc.tensor.matmul(pab[:, :], lhsT=h[:, 2 * g + kc, :], rhs=cs256[:, kc, :],
                                                                       start=(kc == 0), stop=(kc == 1)), [h.b, cs256.b], [pab.b])
            eng = k.act if g % 2 == 0 else k.dve
            if g % 2 == 0:
                k.op(k.act, lambda g=g, pab=pab: nc.scalar.copy(out=ab[:, :, g, :], in_=pab[:, :].rearrange("p (r c) -> p r c", r=2)),
                     [pab.b], [ab.b])
            else:
                k.op(k.dve, lambda g=g, pab=pab: nc.vector.tensor_copy(out=ab[:, :, g, :], in_=pab[:, :].rearrange("p (r c) -> p r c", r=2)),
                     [pab.b], [ab.b])
    def body2(j):
        ab = ABs[j % 2]
        ys = Yst[j % 2]
        A = ab.t[:, 0, :, :].rearrange("p g c -> p (g c)")
        B = ab.t[:, 1, :, :].rearrange("p g c -> p (g c)")
        it = 0
        for ri in range(2):
            for ch in range(2):
                py = psY[it % 2]
                it += 1
                cs_ = slice(ch * 512, (ch + 1) * 512)
                ta, tb = (0, 2) if ri == 0 else (1, 0)
                k.op(k.pe, lambda py=py, ta=ta, cs_=cs_: nc.tensor.matmul(py[:, :], lhsT=ttab[:, j, ta, :], rhs=A[:, cs_],
                                                                         start=True, stop=False), [ttab.b, ab.b], [py.b])
                k.op(k.pe, lambda py=py, tb=tb, cs_=cs_: nc.tensor.matmul(py[:, :], lhsT=ttab[:, j, tb, :], rhs=B[:, cs_],
                                                                         start=False, stop=True), [ttab.b, ab.b], [py.b])
                if it % 2 == 0:
                    k.op(k.act, lambda py=py, ri=ri, cs_=cs_: nc.scalar.copy(out=ys[:, ri, cs_], in_=py[:, :]), [py.b], [ys.b])
                else:
                    k.op(k.dve, lambda py=py, ri=ri, cs_=cs_: nc.vector.tensor_copy(out=ys[:, ri, cs_], in_=py[:, :]), [py.b], [ys.b])
        k.dma(sp, [(ydv[0, j], ys[:, 0, :]), (ydv[1, j], ys[:, 1, :])], k.dsem(ys.b.name), [ys.b], [], store=True)

    load(0)
    load(1)
    load(2)
    pa(0)
    pa(1)
    pb(0)
    for j in range(64):
        if j + 3 < 64:
            load(j + 3)
        if j + 2 < 64:
            pa(j + 2)
        body(j)
        if j + 1 < 64:
            pb(j + 1)
        body2(j)
    k.barrier()
    ph.close()


def phase_odd_out(k, io, l, xin, xout):
    nc = k.nc
    ph = Phase(k)
    half = ph.sb("half", [128, 1], F32)
    w2stk = ph.sb("w2stk", [128, 64], BF16)
    fw = ph.sb("wout", [128, NCH, D], BF16)
    gb = ph.sb("gb0", [128, D], F32)
    gamb = ph.sb("gamb", [128, D], F32)
    betb = ph.sb("betb", [128, D], F32)
    Ystk = [ph.sb(f"Ystk{i}", [128, 2, D], BF16) for i in range(2)]
    fT = [ph.sb(f"fT{i}", [128, NCH, 128], BF16) for i in range(2)]
    xs = [ph.sb(f"xs{i}0", [128, D], F32) for i in range(2)]
    rr = [ph.sb(f"r{i}0", [128, D], F32) for i in range(2)]
    st6 = ph.sb("st60", [128, 2, 6], F32)
    mv = ph.sb("mv0", [128, 2], F32)
    rstd = ph.sb("rstd0", [128, 1], F32)
    psF = [ph.ps(f"psF{i}", [128, NCH, 128], F32) for i in range(2)]
    psy = [[ph.ps(f"psy{i}{nh}", [128, 512], F32) for nh in range(2)] for i in range(2)]
    sp = k.sp
    k.op(k.dve, lambda: nc.vector.memset(half[:, :], -0.5), [], [half.b])
    k.dma(sp, [(w2stk[:, :], io["w2stk"])], k.dsem("w2stk"), [], [w2stk.b])
    Wv = io["fourier_w_out"].rearrange("(c p) n -> p c n", p=128)
    for c in range(0, NCH, 2):
        k.dma(k.pool, [(fw[:, c:c + 2, :], Wv[:, c:c + 2, :])], k.dsem("wout"), [], [fw.b])
    load_bcast(k, gb, io["modd"][l, 0:1, 5 * D:6 * D], 1.0 / ALPHA)
    load_bcast(k, gamb, io["ln_g"][l, 1:2, :])
    load_bcast(k, betb, io["ln_b"][l, 1:2, :])
    xv = xin.rearrange("(k2 k1) d -> k1 k2 d", k1=128)
    ov = xout.rearrange("(k2 k1) d -> k1 k2 d", k1=128)
    ydv = io["yd"]
    fscale = float(1.0 / np.sqrt(L * 256.0))
    bg = BG()

    def loady(m):
        y = Ystk[m % 2]
        k.dma(sp, [(y[0:64, :, :], ydv[0, :, 2 * m:2 * m + 2, :]), (y[64:128, :, :], ydv[1, :, 2 * m:2 * m + 2, :])],
              k.dsem(y.b.name), [], [y.b])

    def loadx(m):
        x = xs[m % 2]
        k.dma(sp, [(x[0:64, :], xv[2 * m]), (x[64:128, :], xv[2 * m + 1])], k.dsem(x.b.name), [], [x.b])

    def stage2(m):
        y = Ystk[m % 2]
        pf = psF[m % 2]
        f = fT[m % 2]
        for par in range(2):
            for c in range(NCH):
                k.op(k.pe, lambda par=par, c=c: nc.tensor.matmul(pf[:, c, par * 64:(par + 1) * 64],
                                                                 lhsT=y[:, par, c * 128:(c + 1) * 128], rhs=w2stk[:, :],
                                                                 start=True, stop=True), [y.b, w2stk.b], [pf.b])
        k.op(k.act, lambda: nc.scalar.mul(out=f[:, 0:4, :], in_=pf[:, 0:4, :], mul=fscale), [pf.b], [f.b])
        k.op(k.dve, lambda: nc.vector.tensor_scalar_mul(out=f[:, 4:8, :], in0=pf[:, 4:8, :], scalar1=fscale), [pf.b], [f.b])

    def wout(m):
        f = fT[m % 2]
        for nh in range(2):
            py = psy[m % 2][nh]
            for c in range(NCH):
                k.op(k.pe, lambda c=c, nh=nh, py=py: nc.tensor.matmul(py[:, :], lhsT=f[:, c, :], rhs=fw[:, c, nh * 512:(nh + 1) * 512],
                                                                     start=(c == 0), stop=(c == NCH - 1)), [f.b, fw.b], [py.b])

    def epi(m):
        r = rr[m % 2]
        yield from epilogue_sub(k, psy[m % 2], xs[m % 2], gb, gamb, betb, r, st6, mv, rstd, half, None, r.b.name, store_pairs=[
            (ov[2 * m], r[0:64, :]), (ov[2 * m + 1], r[64:128, :])])

    loady(0)
    loady(1)
    loadx(0)
    loadx(1)
    stage2(0)
    for m in range(64):
        wout(m)
        if m + 1 < 64:
            stage2(m + 1)
        if m + 2 < 64:
            loady(m + 2)
        run(epi(m))
        if m + 2 < 64:
            loadx(m + 2)
    k.barrier()
    ph.close()


def build(nphases=99, dbg=()):
    nc = bass.Bass("TRN2", target_bir_lowering=False)
    io = {}

    def inp(name, shape, dt=F32):
        io[name] = nc.dram_tensor(name, list(shape), dt, kind="ExternalInput").ap()

    def scr(name, shape, dt=F32):
        io[name] = nc.dram_tensor(name, list(shape), dt, kind="ExternalOutput" if name in dbg else "Internal").ap()

    inp("x", [L, D]); inp("ctx", [CTX, D]); inp("cvecT", [128, NCH, 2])
    inp("ada_w", [2, D, 9 * D]); inp("ada_b", [2, 9 * D])
    inp("ln_g", [2, 3, D]); inp("ln_b", [2, 3, D])
    for nm in ("ffn1_w1", "ffn1_w3", "ffn2_w1", "ffn2_w3"):
        inp(nm, [2, D, FF])
    for nm in ("ffn1_w2", "ffn2_w2"):
        inp(nm, [2, FF, D])
    inp("w_in_all", [D, 1920]); inp("pool_w", [4, 128, 128]); inp("pool_scale", [512]); inp("attn_sinks", [1, 8])
    inp("mix_w_out", [D, D]); inp("fourier_w_out", [D, D])
    inp("ident", [128, 128], BF16)
    inp("ropeT", [NB, 128, 2, 128]); inp("masks", [128, 3, 384], BF16); inp("band", [128, 4, 5, 128], BF16)
    inp("rcnt", [1, 2, 4, 128]); inp("ttab", [128, 64, 3, 128], BF16); inp("cs256", [256, 512], BF16)
    inp("w2stk", [128, 64], BF16)
    io["out"] = nc.dram_tensor("out", [L, D], F32, kind="ExternalOutput").ap()
    scr("modd", [2, 2, 9 * D])
    scr("xa", [L, D]); scr("xb", [L, D]); scr("xc", [L, D]); scr("ctx1", [CTX, D])
    scr("pd", [NB, 128, 512], BF16); scr("qkd", [NB, 128, 5, 128], BF16); scr("vd", [NB, 128, 128], BF16)
    scr("kcd", [128, CTX], BF16); scr("vcd", [2, 128, 128], BF16)
    scr("yd", [2, 64, 128, D], BF16)

    k = Kern(nc)
    F = lambda nm, l: io[nm][l]
    phases = [
        lambda dst: phase_mod(k, io),
        lambda dst: phase_ffn(k, io, 0, F("ffn1_w1", 0), F("ffn1_w3", 0), F("ffn1_w2", 0), 0, 1, 2, 0,
                              io["x"], dst or io["xa"], (io["ctx"], io["ctx1"])),
        lambda dst: phase_even_in(k, io, 0, io["xa"]),
        lambda dst: phase_even_out(k, io, 0, io["xa"], dst or io["xb"]),
        lambda dst: phase_ffn(k, io, 0, F("ffn2_w1", 0), F("ffn2_w3", 0), F("ffn2_w2", 0), 6, 7, 8, 2,
                              io["xb"], dst or io["xc"]),
        lambda dst: phase_ffn(k, io, 1, F("ffn1_w1", 1), F("ffn1_w3", 1), F("ffn1_w2", 1), 0, 1, 2, 0,
                              io["xc"], dst or io["xa"]),
        lambda dst: phase_odd_in(k, io, 1, io["xa"]),
        lambda dst: phase_odd_out(k, io, 1, io["xa"], dst or io["xb"]),
        lambda dst: phase_ffn(k, io, 1, F("ffn2_w1", 1), F("ffn2_w3", 1), F("ffn2_w2", 1), 6, 7, 8, 2,
                              io["xb"], dst or io["out"]),
    ]
    n = min(nphases, len(phases))
    for i in range(n):
        with nc.named_scope(f"phase{i}"):
            phases[i](io["out"] if i == n - 1 and i > 0 else None)
    return nc


def make_consts():
    c = {}
    c["ident"] = np.eye(128, dtype=np.float32).astype(NPBF)
    t = np.arange(L)
    row = (t // 64).astype(np.float64)
    col = (t % 64).astype(np.float64)
    freqs = (10000.0 ** (-np.arange(0, 32, 2, dtype=np.float32) / 32.0)).astype(np.float32).astype(np.float64)
    d = np.arange(64)
    dd = d % 32
    fi = dd % 16
    pos = np.where(d[:, None] < 32, row[None, :], col[None, :])
    ang = (pos.astype(np.float32) * freqs[fi][:, None].astype(np.float32)).astype(np.float64)
    sign = np.where(dd < 16, -1.0, 1.0)[:, None]
    cosT = np.cos(ang)
    sinT = np.sin(ang) * sign
    tab = np.stack([cosT, sinT], axis=1)
    tab = np.concatenate([tab, tab], axis=0)
    c["ropeT"] = np.ascontiguousarray(tab.reshape(128, 2, NB, 128).transpose(2, 0, 1, 3)).astype(np.float32)
    qi = np.arange(128)[:, None]
    ki = np.arange(384)[None, :]
    base = np.abs(ki - 128 - qi) <= 128
    m = np.zeros((128, 3, 384), np.float32)
    m[:, 0] = np.where(base & (ki >= 128), 0.0, NEG)
    m[:, 1] = np.where(base, 0.0, NEG)
    m[:, 2] = np.where(base & (ki < 256), 0.0, NEG)
    c["masks"] = m.astype(NPBF)
    band = np.zeros((128, 4, 5, 128), np.float32)
    rc = np.zeros((1, 2, 4, 128), np.float32)
    src = np.arange(128)[:, None]
    dst = np.arange(128)[None, :]
    for g, w in enumerate((2, 4, 8, 16)):
        hw = w // 2
        inwin = lambda s_glob: ((s_glob >= dst - hw) & (s_glob <= dst + hw - 1)).astype(np.float32)
        eye = (src == dst).astype(np.float32)
        band[:, g, 0] = inwin(src - 128)
        band[:, g, 1] = inwin(src) - w * eye
        band[:, g, 2] = inwin(src + 128)
        cnt_first = np.minimum(dst + hw, L) - np.maximum(dst - hw, 0)
        tg = (L - 128) + dst
        cnt_last = np.minimum(tg + hw, L) - np.maximum(tg - hw, 0)
        band[:, g, 3] = inwin(src) - cnt_first * eye
        band[:, g, 4] = inwin(src) - cnt_last * eye
        rc[0, 0, g] = 1.0 / cnt_first[0]
        rc[0, 1, g] = 1.0 / cnt_last[0]
    c["band"] = band.astype(NPBF)
    c["rcnt"] = rc
    n1 = np.arange(128)[:, None, None]
    j = np.arange(64)[None, :, None]
    k1 = np.arange(128)[None, None, :]
    th = 2.0 * np.pi * (((64 * n1 + j) * k1) % L) / L
    c["ttab"] = np.stack([np.cos(th), np.sin(th), -np.sin(th)], axis=2).astype(np.float32).astype(NPBF)
    cc = np.arange(256)
    th2 = 2.0 * np.pi * ((cc[:, None] * cc[None, :]) % 256) / 256.0
    c["cs256"] = np.concatenate([np.cos(th2), np.sin(th2)], axis=1).astype(np.float32).astype(NPBF)
    jj = np.arange(64)
    th3 = 2.0 * np.pi * ((jj[:, None] * jj[None, :]) % 64) / 64.0
    c["w2stk"] = np.concatenate([np.cos(th3), -np.sin(th3)], axis=0).astype(np.float32).astype(NPBF)
    return c


def layout_w_in(w_in):
    w_in = np.asarray(w_in, dtype=np.float32)
    p = w_in[:, 0:512]
    q = w_in[:, 512:1024].reshape(D, 8, 64)
    kk = w_in[:, 1024:1152].reshape(D, 2, 64)
    v = w_in[:, 1152:1280]
    pairs = [np.concatenate([q[:, i], q[:, i + 4]], axis=1) for i in range(4)] + [np.concatenate([kk[:, 0], kk[:, 1]], axis=1)]
    qk = np.concatenate(pairs, axis=1)
    d = np.arange(64)
    partner = np.where((d % 32) < 16, d + 16, d - 16)
    idx = np.concatenate([blk * 64 + partner for blk in range(10)])
    qksw = qk[:, idx]
    return np.ascontiguousarray(np.concatenate([p, qk, qksw, v], axis=1))


def make_in_maps(inputs, cores=range(8)):
    consts = make_consts()
    f32 = lambda a: np.ascontiguousarray(np.asarray(a, dtype=np.float32))
    shared = {}
    for nm in ("ada_w", "ada_b", "ln_g", "ln_b", "ffn1_w1", "ffn1_w3", "ffn1_w2", "ffn2_w1", "ffn2_w3", "ffn2_w2"):
        shared[nm] = f32(inputs[nm])
    shared["w_in_all"] = layout_w_in(inputs["mix_w_in"][0])
    shared["pool_w"] = f32(inputs["pool_w"][0])
    shared["pool_scale"] = f32(inputs["pool_scale"][0])
    shared["attn_sinks"] = f32(inputs["attn_sinks"][0]).reshape(1, 8)
    shared["mix_w_out"] = f32(inputs["mix_w_out"][0])
    shared["fourier_w_out"] = f32(inputs["fourier_w_out"][0])
    shared.update(consts)
    x = inputs["x"]; ctx = inputs["ctx"]; c = f32(inputs["c"]); cc = f32(inputs["c_ctx"])
    in_maps = []
    for b in cores:
        m = dict(shared)
        m["x"] = f32(x[b])
        m["ctx"] = f32(ctx[b])
        cv = np.stack([c[b], cc], axis=0)
        m["cvecT"] = np.ascontiguousarray(cv.reshape(2, NCH, 128).transpose(2, 1, 0))
        in_maps.append(m)
    return in_maps


def kernel(**inputs):
    nc = build()
    in_maps = make_in_maps(inputs)
    res = run_bass_kernel_spmd(nc, in_maps, core_ids=list(range(8)))
    return np.stack([r["out"] for r in res.results], axis=0)
```
